# Optimizing a Trainium2 kernel written in Bass

```python
import jax, jax.numpy as jnp
from jax import lax
import numpy as np

D_MODEL = 2048
BATCH = 4
SEQ = 2048
DEPTH = 1
DEC_BATCH = 128
DEC_SEQ = 1
PAST_LEN = 16384
PAGE_SIZE = 128

A_W = 1024
A_GROUPS = 8
A_GROUP_W = A_W // A_GROUPS
CHUNK = 128
B_HEADS = 32
HEAD_SIZE = 64
B_W = B_HEADS * HEAD_SIZE
DECAY_LORA = 96
AAA_LORA = 96
SHIFT_W = 3 * B_W + DECAY_LORA + AAA_LORA
IN_SIZES = (A_W, A_W, A_W, SHIFT_W, B_W, D_MODEL, D_MODEL)
SHIFT_SIZES = (B_W, DECAY_LORA, B_W, B_W, AAA_LORA)
IN_W = 3 * A_W + SHIFT_W + B_W + 2 * D_MODEL
NORM_EPS = 1e-6
GN_EPS = HEAD_SIZE * 1e-5

kernel_name = "hybrid_gmlp_rwkv7_gated_step"


def _split(t, sizes):
    out, o = [], 0
    for s in sizes:
        out.append(t[..., o:o + s])
        o += s
    return out


def _rmsnorm(x, g):
    xf = x.astype(jnp.float32)
    y = xf * lax.rsqrt(jnp.mean(xf * xf, axis=-1, keepdims=True) + NORM_EPS)
    return (y * g.astype(jnp.float32)).astype(x.dtype)


def _layernorm(x, g, b):
    xf = x.astype(jnp.float32)
    mu = jnp.mean(xf, axis=-1, keepdims=True)
    var = jnp.mean(jnp.square(xf - mu), axis=-1, keepdims=True)
    y = (xf - mu) * lax.rsqrt(var + NORM_EPS)
    return (y * g.astype(jnp.float32) + b.astype(jnp.float32)).astype(x.dtype)


def _chunk_mix(v, w_s, b_s):
    bsz, T, _ = v.shape
    n_chunks = -(-T // CHUNK)
    tp = n_chunks * CHUNK
    vp = jnp.pad(v, ((0, 0), (0, tp - T), (0, 0)))
    vp = vp.reshape(bsz, n_chunks, CHUNK, A_GROUPS, A_GROUP_W)
    causal = jnp.tril(jnp.ones((CHUNK, CHUNK), dtype=bool))
    wm = jnp.where(causal[None], w_s, 0)
    out = jnp.einsum('gts,bnsgc->bntgc', wm, vp) + b_s.T[None, None, :, :, None]
    return out.reshape(bsz, tp, A_W)[:, :T]


def _wkv_scan(s0, r, w, k, v, a, b):
    def step(s, inp):
        r_t, w_t, k_t, v_t, a_t, b_t = inp
        sa = jnp.einsum('bhij,bhj->bhi', s, a_t)
        s = s * w_t[:, :, None, :] + sa[..., None] * b_t[:, :, None, :] + v_t[..., None] * k_t[:, :, None, :]
        y_t = jnp.einsum('bhij,bhj->bhi', s, r_t)
        return s, y_t
    xs = tuple(jnp.swapaxes(t.astype(jnp.float32), 0, 1) for t in (r, w, k, v, a, b))
    s, ys = lax.scan(step, s0.astype(jnp.float32), xs)
    return s, jnp.swapaxes(ys, 0, 1)


def _layer(x, c, wkv0, shift0, norm_g, w_c, b_c, w_in, ln_v_g, ln_v_b, w_s, b_s,
           mu_shift, w0, w2, a0, a2, k_k, k_a, r_k, gn_g, gn_b, p_a, p_b, w_out):
    bsz, T, _ = x.shape
    f32 = jnp.float32
    mod = c @ w_c + b_c
    c_shift, c_scale, c_gate = jnp.split(mod, 3, axis=-1)
    h = _rmsnorm(x, norm_g) * (1 + c_scale[:, None]) + c_shift[:, None]
    proj = jnp.einsum('btd,de->bte', h, w_in)
    u_a, v_a, z_a, p_rwkv, z_b, g_a, g_b = _split(proj, IN_SIZES)

    u = jax.nn.gelu(u_a)
    v = _layernorm(jax.nn.gelu(v_a), ln_v_g, ln_v_b)
    y_a = u * _chunk_mix(v, w_s, b_s) * jax.nn.silu(z_a)

    prev = jnp.concatenate([shift0[:, None].astype(p_rwkv.dtype), p_rwkv[:, :-1]], axis=1)
    ps = p_rwkv + mu_shift * (prev - p_rwkv)
    r, wd, k, vv, ad = _split(ps, SHIFT_SIZES)
    w_log = -jax.nn.softplus(-(w0 + jnp.tanh(wd) @ w2).astype(f32)) - 0.5
    decay = jnp.exp(-jnp.exp(w_log))
    a = jax.nn.sigmoid((a0 + ad @ a2).astype(f32))
    kk = (k * k_k).astype(f32).reshape(bsz, T, B_HEADS, HEAD_SIZE)
    kk = kk / jnp.maximum(jnp.linalg.norm(kk, axis=-1, keepdims=True), 1e-12)
    k = k.astype(f32) * (1 + (a - 1) * k_a.astype(f32))
    hd = lambda t: t.astype(f32).reshape(bsz, T, B_HEADS, HEAD_SIZE)
    r_h, w_h, k_h, v_h, a_h = hd(r), hd(decay), hd(k), hd(vv), hd(a)
    s_new, y = _wkv_scan(wkv0, r_h, w_h, k_h, v_h, -kk, kk * a_h)
    mu = jnp.mean(y, axis=-1, keepdims=True)
    var = jnp.mean(jnp.square(y - mu), axis=-1, keepdims=True)
    y = (y - mu) * lax.rsqrt(var + GN_EPS)
    y = y * gn_g.astype(f32).reshape(B_HEADS, HEAD_SIZE) + gn_b.astype(f32).reshape(B_HEADS, HEAD_SIZE)
    bonus = jnp.sum(r_h * k_h * r_k.astype(f32), axis=-1, keepdims=True) * v_h
    y_b = (y + bonus).reshape(bsz, T, B_W).astype(x.dtype) * jax.nn.silu(z_b)

    merged = jax.nn.sigmoid(g_a) * (y_a @ p_a) + jax.nn.sigmoid(g_b) * (y_b @ p_b)
    out = x + c_gate[:, None] * (merged @ w_out)
    return out, s_new, p_rwkv[:, -1], v


def setup_inputs(seed: int = 0) -> dict:
    key = jax.random.key(seed)
    ks = jax.random.split(key, 40)
    n = lambda i, shape: jax.random.normal(ks[i], shape, jnp.float32)
    L = DEPTH
    return {
        "x_prompt": n(0, (BATCH, SEQ, D_MODEL)),
        "x_sample": n(1, (DEC_BATCH, DEC_SEQ, D_MODEL)),
        "c_prompt": n(2, (BATCH, D_MODEL)),
        "c_sample": n(3, (DEC_BATCH, D_MODEL)),
        "state_wkv": 0.1 * n(4, (L, DEC_BATCH, B_HEADS, HEAD_SIZE, HEAD_SIZE)),
        "state_shift": n(5, (L, DEC_BATCH, SHIFT_W)),
        "norm_g": 1.0 + 0.02 * n(6, (L, D_MODEL)),
        "w_c": 0.5 * D_MODEL ** -0.5 * n(7, (L, D_MODEL, 3 * D_MODEL)),
        "b_c": 0.02 * n(8, (L, 3 * D_MODEL)),
        "w_in": D_MODEL ** -0.5 * n(9, (L, D_MODEL, IN_W)),
        "ln_v_g": 1.0 + 0.02 * n(10, (L, A_W)),
        "ln_v_b": 0.02 * n(11, (L, A_W)),
        "w_s": CHUNK ** -0.5 * n(12, (L, A_GROUPS, CHUNK, CHUNK)),
        "b_s": 1.0 + 0.1 * n(13, (L, A_GROUPS, CHUNK)),
        "mu_shift": jax.random.uniform(ks[14], (L, SHIFT_W), jnp.float32),
        "w0": jax.random.uniform(ks[15], (L, B_W), jnp.float32, -4.0, 1.0),
        "w2": 0.1 * DECAY_LORA ** -0.5 * n(16, (L, DECAY_LORA, B_W)),
        "a0": 0.1 * n(17, (L, B_W)),
        "a2": 0.1 * AAA_LORA ** -0.5 * n(18, (L, AAA_LORA, B_W)),
        "k_k": 0.85 + 0.05 * n(19, (L, B_W)),
        "k_a": 1.0 + 0.05 * n(20, (L, B_W)),
        "r_k": 0.1 * n(21, (L, B_HEADS, HEAD_SIZE)),
        "gn_g": 1.0 + 0.02 * n(22, (L, B_W)),
        "gn_b": 0.02 * n(23, (L, B_W)),
        "p_a": A_W ** -0.5 * n(24, (L, A_W, D_MODEL)),
        "p_b": B_W ** -0.5 * n(25, (L, B_W, D_MODEL)),
        "w_out": D_MODEL ** -0.5 * n(26, (L, D_MODEL, D_MODEL)),
        "final_g": 1.0 + 0.02 * n(27, (D_MODEL,)),
    }


def reference(x_prompt, x_sample, c_prompt, c_sample, state_wkv, state_shift,
              norm_g, w_c, b_c, w_in, ln_v_g, ln_v_b, w_s, b_s,
              mu_shift, w0, w2, a0, a2, k_k, k_a, r_k, gn_g, gn_b,
              p_a, p_b, w_out, final_g):
    bp = x_prompt.shape[0]
    wkv_zero = jnp.zeros((bp, B_HEADS, HEAD_SIZE, HEAD_SIZE), jnp.float32)
    shift_zero = jnp.zeros((bp, SHIFT_W), x_prompt.dtype)
    hp, hs = x_prompt, x_sample
    wkv_p, sh_p, wkv_s, sh_s, cv_s = [], [], [], [], []
    for l in range(DEPTH):
        params = (norm_g[l], w_c[l], b_c[l], w_in[l], ln_v_g[l], ln_v_b[l], w_s[l], b_s[l],
                  mu_shift[l], w0[l], w2[l], a0[l], a2[l], k_k[l], k_a[l], r_k[l],
                  gn_g[l], gn_b[l], p_a[l], p_b[l], w_out[l])
        hp, s_p, shp, _ = _layer(hp, c_prompt, wkv_zero, shift_zero, *params)
        hs, s_s, shs, v_s = _layer(hs, c_sample, state_wkv[l], state_shift[l], *params)
        wkv_p.append(s_p); sh_p.append(shp)
        wkv_s.append(s_s); sh_s.append(shs); cv_s.append(v_s)
    y_prompt = _rmsnorm(hp, final_g)
    y_sample = _rmsnorm(hs, final_g)
    wkv_prompt = jnp.stack(wkv_p)
    shift_prompt = jnp.stack(sh_p)
    wkv_sample = jnp.stack(wkv_s)
    shift_sample = jnp.stack(sh_s)
    chunk_v_sample = jnp.stack(cv_s)
    return (y_prompt, y_sample, wkv_prompt, shift_prompt, wkv_sample, shift_sample, chunk_v_sample)
```

```python
import numpy as np
import concourse.bass as bass
import concourse.mybir as mybir

F32 = mybir.dt.float32
BF16 = mybir.dt.bfloat16
AF = mybir.ActivationFunctionType
ALU = mybir.AluOpType
AX = mybir.AxisListType


class Buf:
    __slots__ = ("t", "w", "r", "name", "excl")

    def __init__(self, t, name=""):
        self.t = t
        self.excl = False
        self.w = {}
        self.r = {}
        self.name = name

    def __getitem__(self, idx):
        return self.t[idx]


class Eng:
    def __init__(self, fw, name, h, sem):
        self.fw = fw
        self.name = name
        self.h = h
        self.sem = sem
        self.cnt = 0
        self.waited = {}
        self.pend_r = []
        self.pend_w = []


class FW:
    def __init__(self, nc, n_dma_sems=12):
        self.nc = nc
        self._ctx = []
        self.sems = {}
        self.E = {}
        for name, h in (("pe", nc.tensor), ("act", nc.scalar), ("dve", nc.vector),
                        ("pool", nc.gpsimd), ("sp", nc.sync)):
            s = self.enter(nc.semaphore("sem_" + name))
            self.sems[id(s)] = s
            self.E[name] = Eng(self, name, h, s)
        self.dma_ring = {}
        for q in ("sp", "pool", "act"):
            ring = []
            for i in range(n_dma_sems):
                s = self.enter(nc.semaphore(f"dsem_{q}{i}"))
                self.sems[id(s)] = s
                ring.append([s, 0])
            self.dma_ring[q] = [ring, 0]
        self.n_inst = 0

    def enter(self, cm):
        v = cm.__enter__()
        self._ctx.append(cm)
        return v

    def mark(self):
        return len(self._ctx)

    def release(self, m):
        self.barrier()
        while len(self._ctx) > m:
            self._ctx.pop().__exit__(None, None, None)

    def close(self):
        for cm in reversed(self._ctx):
            cm.__exit__(None, None, None)
        self._ctx = []

    def sb(self, name, shape, dt=F32):
        self._uid = getattr(self, "_uid", 0) + 1
        return Buf(self.enter(self.nc.sbuf_tensor(f"{name}_{self._uid}", list(shape), dt)), name)

    def barrier(self):
        for e in self.E.values():
            for o in self.E.values():
                if o.cnt > 0:
                    self._wait(e, o.sem, o.cnt)
            for q, (ring, pos) in self.dma_ring.items():
                for sem, val in ring:
                    if val > 0:
                        self._wait(e, sem, val)

    def ps(self, name, shape, dt=F32):
        b = Buf(self.enter(self.nc.psum_tensor(name, list(shape), dt)), name)
        b.excl = True
        return b

    def _wait(self, e, sem, val):
        k = id(sem)
        if e.waited.get(k, 0) >= val:
            return
        e.h.wait_ge(sem, val)
        e.waited[k] = val
        self.n_inst += 1

    def _deps(self, e, reads, writes, waw=True):
        need = {}
        for b in reads:
            for k, v in b.w.items():
                if need.get(k, 0) < v:
                    need[k] = v
            if b.excl:
                own = id(e.sem)
                for k, v in b.r.items():
                    if k != own and need.get(k, 0) < v:
                        need[k] = v
        for b in writes:
            for k, v in b.r.items():
                if need.get(k, 0) < v:
                    need[k] = v
            if waw:
                for k, v in b.w.items():
                    if need.get(k, 0) < v:
                        need[k] = v
        for k, v in need.items():
            self._wait(e, self.sems[k], v)

    def op(self, eng, fn, reads=(), writes=(), inc=True, waw=True):
        e = self.E[eng]
        self._deps(e, reads, writes, waw)
        ins = fn()
        self.n_inst += 1
        if inc:
            e.cnt += 1
            ins.then_inc(e.sem, 1)
            k = id(e.sem)
            rs = list(reads) + e.pend_r
            ws = list(writes) + e.pend_w
            e.pend_r = []
            e.pend_w = []
            for b in rs:
                b.r[k] = e.cnt
            for b in ws:
                if waw and b in writes:
                    pass
                b.w[k] = e.cnt
        else:
            e.pend_r += list(reads)
            e.pend_w += list(writes)
        return ins

    def fresh(self, b):
        pass

    def dma(self, q, out_ap, in_ap, reads=(), writes=(), waw=True, **kw):
        e = self.E[q]
        ring, pos = self.dma_ring[q]
        slot = ring[pos % len(ring)]
        self.dma_ring[q][1] = pos + 1
        sem, val = slot
        if val > 0:
            self._wait(e, sem, val)
        self._deps(e, reads, writes, waw)
        ins = e.h.dma_start(out=out_ap, in_=in_ap, **kw)
        self.n_inst += 1
        slot[1] = val + 16
        ins.then_inc(sem, 16)
        k = id(sem)
        for b in reads:
            b.r[k] = slot[1]
        for b in writes:
            b.w[k] = slot[1]
        return ins

    def finish(self):
        e = self.E["sp"]
        for q, (ring, pos) in self.dma_ring.items():
            for sem, val in ring:
                if val > 0:
                    self._wait(e, sem, val)
        for name, en in self.E.items():
            if name != "sp" and en.cnt > 0:
                self._wait(e, en.sem, en.cnt)
from concourse.bass_utils import run_bass_kernel_spmd
TO = 512
TP = 512
TS = 16
D = 2048
NCH = 64
NF = TO + TS
C_R, C_WD, C_K, C_V, C_AD = 3072, 5120, 5216, 7264, 9312
C_ZB, C_GA, C_GB = 9408, 11456, 13504
EXPM05 = 0.6065306597126334
GN_EPS = 64e-5
NORM_EPS = 1e-6
PC_NORMG, PC_MUR, PC_MUK, PC_MUV, PC_W0, PC_A0, PC_KK, PC_KA, PC_RK, PC_GNG, PC_GNB = [16 * i for i in range(11)]
PC_MUWD, PC_MUAD = 176, 177
NPC = 178


def blocks(c0, c1, w=512):
    out = []
    a = c0
    while a < c1:
        b = min(a + w, c1)
        out.append((a, b))
        a = b
    return out


class StopBuild(Exception):
    pass


def build_program():
    nc = bass.Bass("TRN2", target_bir_lowering=False)
    fw = FW(nc)
    V, A, G, T = nc.vector, nc.scalar, nc.gpsimd, nc.tensor

    def din(name, shape, dt=F32):
        return Buf(nc.dram_tensor(name, list(shape), dt, kind="ExternalInput").ap(), name)

    def dout(name, shape, dt=F32):
        return Buf(nc.dram_tensor(name, list(shape), dt, kind="ExternalOutput").ap(), name)

    x_pre = din("x_pre", [1024, D]); x_own = din("x_own", [1024, D]); x_misc = din("x_misc", [4, 17, D])
    cin = din("cin", [17, D]); maskv = din("maskv", [128, 1]); pc_d = din("pc", [128, NPC])
    swkv = din("swkv", [16, 128, TS, 64]); sshift = din("sshift", [TS, 6336])
    w_c = din("w_c", [D, 6144]); b_c = din("b_c", [1, 6144]); w_in = din("w_in", [D, 15552])
    lnv_g = din("ln_v_g", [1, 1024]); lnv_b = din("ln_v_b", [1, 1024])
    w_sT = din("w_sT", [8, 128, 128]); b_s = din("b_s", [1, 1024]); ws00 = din("ws00", [1, 8]); bs0 = din("bs0", [1, 8])
    w2 = din("w2", [96, D]); a2 = din("a2", [96, D])
    p_a = din("p_a", [1024, D]); p_b = din("p_b", [D, D]); w_out = din("w_out", [D, D]); final_g = din("final_g", [1, D])

    y_own = dout("y_own", [1024, D]); y_s = dout("y_s", [TS, D])
    wkv_p = dout("wkv_p", [16, 128, 64]); sh_p = dout("sh_p", [128, 50])
    wkv_s = dout("wkv_s", [16, 128, TS, 64]); sh_s = dout("sh_s", [128, 50, TS]); cv_s = dout("cv_s", [TS, 1024])

    def tt(eng, out, in0, in1, op, R, W, **kw):
        h = fw.E[eng].h
        return fw.op(eng, lambda: h.tensor_tensor(out, in0, in1, op), reads=R, writes=W, **kw)

    def ts(eng, out, in0, s1, s2, op0, op1, R, W, **kw):
        h = fw.E[eng].h
        if op1 is None:
            return fw.op(eng, lambda: h.tensor_scalar(out, in0, s1, None, op0), reads=R, writes=W, **kw)
        return fw.op(eng, lambda: h.tensor_scalar(out, in0, s1, s2, op0, op1), reads=R, writes=W, **kw)

    def stt(out, in0, sc, in1, op0, op1, R, W, **kw):
        return fw.op("dve", lambda: V.scalar_tensor_tensor(out, in0, sc, in1, op0, op1), reads=R, writes=W, **kw)

    def act(out, in_, func, R, W, bias=None, scale=None, **kw):
        def f():
            kws = {}
            if bias is not None:
                kws["bias"] = bias
            if scale is not None:
                kws["scale"] = scale
            return A.activation(out, in_, func, **kws)
        return fw.op("act", f, reads=R, writes=W, **kw)

    def cp(eng, out, in_, R, W, **kw):
        if eng == "act":
            return fw.op("act", lambda: A.copy(out, in_), reads=R, writes=W, **kw)
        h = fw.E[eng].h
        return fw.op(eng, lambda: h.tensor_copy(out, in_), reads=R, writes=W, **kw)

    def mm(out, lhsT, rhs, start, stop, R, W, inc=None):
        if inc is None:
            inc = stop
        return fw.op("pe", lambda: T.matmul(out, lhsT, rhs, start=start, stop=stop), reads=R, writes=W, inc=inc, waw=start)

    ident = fw.sb("ident", [128, 128]); ones_t = fw.sb("ones_t", [128, 128])
    bones = fw.sb("bones", [128, 128]); bones64 = fw.sb("bones64", [128, 128])
    ident2 = fw.sb("ident2", [128, 64])
    mask5 = fw.sb("mask5", [128, 5, 64]); triu = fw.sb("triu", [128, 128])
    zero1 = fw.sb("zero1", [128, 1])
    fw.op("pool", lambda: G.memset(ones_t[:], 1.0), writes=[ones_t])
    fw.op("pool", lambda: G.memset(zero1[:], 0.0), writes=[zero1])
    fw.op("pool", lambda: G.affine_select(out=ident[:], in_=ones_t[:], pattern=[[1, 128]], compare_op=ALU.is_equal,
                                          fill=0.0, base=0, channel_multiplier=-1), reads=[ones_t], writes=[ident])
    fw.op("pool", lambda: G.affine_select(out=triu[:], in_=ones_t[:], pattern=[[1, 128]], compare_op=ALU.is_ge,
                                          fill=0.0, base=0, channel_multiplier=-1), reads=[ones_t], writes=[triu])
    fw.op("pool", lambda: G.memset(bones[:], 0.0), writes=[bones])
    fw.op("pool", lambda: G.memset(bones[0:64, 0:64], 1.0), writes=[bones], waw=True)
    fw.op("pool", lambda: G.memset(bones[64:128, 64:128], 1.0), writes=[bones], waw=True)
    ts("pool", bones64[:], bones[:], 1.0 / 64.0, None, ALU.mult, None, [bones], [bones64])
    bones_bf = fw.sb("bones_bf", [128, 128], BF16)
    cp("pool", bones_bf[:], bones[:], [bones], [bones_bf])
    tt("pool", ident2[:], ident[:, 0:64], ident[:, 64:128], ALU.add, [ident], [ident2])
    for hp_ in range(2):
        ps_ = slice(64 * hp_, 64 * hp_ + 64)
        for x_ in range(5):
            cmp_, pat, cm = {0: (ALU.is_gt, 1, -1), 1: (ALU.is_ge, 1, -1), 2: (ALU.is_gt, 1, -1),
                             3: (ALU.is_ge, 1, -1), 4: (ALU.is_gt, -1, 1)}[x_]
            fw.op("pool", lambda ps_=ps_, x_=x_, cmp_=cmp_, pat=pat, cm=cm: G.affine_select(
                out=mask5[ps_, x_, :], in_=ones_t[ps_, 0:64], pattern=[[pat, 64]], compare_op=cmp_,
                fill=0.0, base=0, channel_multiplier=cm), reads=[ones_t], writes=[mask5], waw=False)

    import os
    STOP = os.environ.get("KSTOP", "")
    def done():
        fw.finish(); fw.close(); return nc
    if STOP == "k1":
        return done()
    pc = fw.sb("pc_sb", [128, NPC]); mk = fw.sb("mk", [128, 1])
    fw.dma("sp", pc[:], pc_d[:], reads=[pc_d], writes=[pc])
    fw.dma("sp", mk[:], maskv[:], reads=[maskv], writes=[mk])
    gneps = fw.sb("gneps", [128, 1]); neps = fw.sb("neps", [128, 1]); one1 = fw.sb("one1", [128, 1])
    fw.op("pool", lambda: G.memset(gneps[:], GN_EPS), writes=[gneps])
    fw.op("pool", lambda: G.memset(neps[:], NORM_EPS), writes=[neps])
    fw.op("pool", lambda: G.memset(one1[:], 1.0), writes=[one1])

    if STOP == "k2":
        return done()
    PJ = [fw.ps(f"pj{i}", [128, 512]) for i in range(3)]
    WK = [fw.ps(f"wk{i}", [128, 512]) for i in range(5)]

    wring = [fw.sb(f"wslot{i}", [128, 16, 128], BF16) for i in range(5)]
    wpos = [0]

    def load_w(src, c0, width=128, rows=D):
        slot = wring[wpos[0] % len(wring)]; wpos[0] += 1
        nk = rows // 128
        fw.dma("pool", slot[:, 0:nk, 0:width], src[0:rows, c0:c0 + width].rearrange("(k p) n -> p k n", p=128),
               reads=[src], writes=[slot])
        return slot

    def inproj(pss, wt, coff, hT, c0, c1, nk=16):
        for bi, (a, b) in enumerate(blocks(c0, c1)):
            for k in range(nk):
                mm(pss[bi][:, 0:b - a], wt[:, k, coff:coff + 128], hT[:, k, a:b], k == 0, k == nk - 1,
                   [wt, hT], [pss[bi]])
    sel16 = fw.sb("sel16", [17, 128])
    fw.op("pool", lambda: G.affine_select(out=sel16[:], in_=ones_t[0:17, :], pattern=[[0, 128]], compare_op=ALU.is_ge,
                                          fill=0.0, base=-16, channel_multiplier=1), reads=[ones_t], writes=[sel16])
    modT = fw.sb("modT", [128, 32, 17]); gsc = fw.sb("gsc", [128, 16, 17]); gate_bc = fw.sb("gate_bc", [128, D]); gate_s = fw.sb("gate_s", [16, D])
    m0 = fw.mark()
    cin_sb = fw.sb("cin_sb", [17, D])
    fw.dma("sp", cin_sb[:], cin[:], reads=[cin], writes=[cin_sb])
    cT = fw.sb("cT", [128, 16, 17], BF16)
    for k in range(16):
        fw.op("pe", lambda k=k: T.transpose(WK[0][:, k * 17:(k + 1) * 17], cin_sb[0:17, k * 128:(k + 1) * 128], ident[0:17, 0:17]),
              reads=[cin_sb, ident], writes=[WK[0]], waw=(k == 0), inc=(k == 15))
    cp("dve", cT[:], WK[0][:, 0:16 * 17].rearrange("p (k t) -> p k t", t=17), [WK[0]], [cT])
    if STOP == "k3":
        return done()
    mod = fw.sb("mod", [17, 6144])
    bcr = [fw.sb(f"bcr{i}", [17, 512]) for i in range(2)]
    wbig0 = [fw.sb(f"wbig0_{i}", [128, 16, 512], BF16) for i in range(2)]
    for cb in range(12):
        bcb = bcr[cb % 2]
        fw.dma("sp", bcb[:], b_c[0:1, cb * 512:(cb + 1) * 512].to_broadcast([17, 512]), reads=[b_c], writes=[bcb])
        ps = PJ[cb % 3]
        wt = wbig0[cb % 2]
        fw.dma("pool", wt[:, :, :], w_c[:, cb * 512:(cb + 1) * 512].rearrange("(k p) n -> p k n", p=128), reads=[w_c], writes=[wt])
        for k in range(16):
            mm(ps[0:17, :], cT[:, k, :], wt[:, k, :], k == 0, k == 15, [cT, wt], [ps])
        tt("dve", mod[:, cb * 512:(cb + 1) * 512], ps[0:17, :], bcb[:], ALU.add, [ps, bcb], [mod], waw=False)
    if STOP == "k4":
        return done()
    for half in range(2):
        ps = WK[1 + half]
        for t in range(16):
            tg = half * 16 + t
            fw.op("pe", lambda t=t, tg=tg, ps=ps: T.transpose(ps[:, t * 17:(t + 1) * 17], mod[0:17, tg * 128:(tg + 1) * 128], ident[0:17, 0:17]),
                  reads=[mod, ident], writes=[ps], waw=(t == 0), inc=(t == 15))
        cp("dve", modT[:, half * 16:(half + 1) * 16, :], ps[:, 0:16 * 17].rearrange("p (k t) -> p k t", t=17), [ps], [modT], waw=False)
    if STOP == "k5":
        return done()
    ts("dve", gsc[:], modT[:, 16:32, :], 1.0, None, ALU.add, None, [modT], [gsc])
    tt("dve", gsc[:], gsc[:], pc[:, PC_NORMG:PC_NORMG + 16].unsqueeze(2).to_broadcast([128, 16, 17]), ALU.mult, [gsc, pc], [gsc])
    if STOP == "k6":
        return done()
    for cb in range(4):
        ps = PJ[cb % 3]
        mm(ps[:, :], sel16[:, :], mod[0:17, 4096 + cb * 512:4096 + (cb + 1) * 512], True, True, [sel16, mod], [ps])
        cp("act", gate_bc[:, cb * 512:(cb + 1) * 512], ps[:, :], [ps], [gate_bc], waw=False)

    if STOP == "k7":
        return done()
    cp("dve", gate_s[:], mod[0:16, 4096:6144], [mod], [gate_s])
    if STOP == "k8":
        return done()
    if STOP == "k9":
        fw.barrier()
        return done()
    if STOP == "k10":
        while len(fw._ctx) > m0:
            fw._ctx.pop().__exit__(None, None, None)
        return done()
    fw.release(m0)
    xpos = [0]
    XH = {}

    def alloc_x():
        XH["xt"] = fw.sb("xt", [128, D]); XH["rs"] = [fw.sb(f"rs{i}", [128, 4]) for i in range(2)]
        XH["junk"] = fw.sb("junk", [128, D], BF16); XH["xt_tmp"] = fw.sb("xt_tmp", [128, 16])

    def build_hT(hT, col0, xsrc, r0, nrows, is_misc=False, ns=16, xsel=None):
        xt = XH["xt"]; rs = XH["rs"][xpos[0] % 2]; xpos[0] += 1; junk = XH["junk"]; xt_tmp = XH["xt_tmp"]
        n = nrows
        fw.dma("sp", xt[0:n, :], (xsrc[xsel] if xsel is not None else xsrc[r0:r0 + n, :]), reads=[xsrc], writes=[xt])
        if STOP == "hs1":
            raise StopBuild()
        fw.op("act", lambda: A.activation(junk[0:n, :], xt[0:n, :], AF.Square, accum_out=rs[0:n, 0:1]), reads=[xt], writes=[junk, rs])
        if STOP == "hs2":
            raise StopBuild()
        ts("dve", rs[0:n, 1:2], rs[0:n, 0:1], 1.0 / D, NORM_EPS, ALU.mult, ALU.add, [rs], [rs])
        act(rs[0:n, 2:3], rs[0:n, 1:2], AF.Sqrt, [rs], [rs])
        fw.op("dve", lambda: V.reciprocal(rs[0:n, 3:4], rs[0:n, 2:3]), reads=[rs], writes=[rs])
        if STOP == "hs3":
            raise StopBuild()
        ts("dve", xt[0:n, :], xt[0:n, :], rs[0:n, 3:4], None, ALU.mult, None, [xt, rs], [xt])
        if STOP == "hs4":
            raise StopBuild()
        for q in range(4):
            ps = PJ[q] if q < 3 else WK[0]
            for kk_ in range(4):
                k = q * 4 + kk_
                fw.op("pe", lambda k=k, kk_=kk_, ps=ps: T.transpose(ps[:, kk_ * 128:kk_ * 128 + n], xt[0:n, k * 128:(k + 1) * 128], ident[0:n, 0:n]),
                      reads=[xt, ident], writes=[ps], waw=(kk_ == 0), inc=(kk_ == 3))
            if STOP == "hs5":
                raise StopBuild()
            for kk_ in range(4):
                k = q * 4 + kk_
                eng = "dve" if q % 2 == 0 else "pool"
                if not is_misc:
                    if eng == "pool":
                        act(hT[:, k, col0:col0 + n], ps[:, kk_ * 128:kk_ * 128 + n], AF.Identity, [ps, gsc, modT], [hT],
                            bias=modT[:, k, 16:17], scale=gsc[:, k, 16:17], waw=False)
                    else:
                        ts("dve", hT[:, k, col0:col0 + n], ps[:, kk_ * 128:kk_ * 128 + n], gsc[:, k, 16:17], modT[:, k, 16:17],
                           ALU.mult, ALU.add, [ps, gsc, modT], [hT], waw=False)
                else:
                    if ns:
                        tt("dve", xt_tmp[:, 0:16], ps[:, kk_ * 128:kk_ * 128 + 16], gsc[:, k, 0:16], ALU.mult, [ps, gsc], [xt_tmp])
                        tt("dve", hT[:, k, col0:col0 + 16], xt_tmp[:, 0:16], modT[:, k, 0:16], ALU.add, [xt_tmp, modT], [hT], waw=False)
                    ts("dve", hT[:, k, 0:1], ps[:, kk_ * 128 + 16:kk_ * 128 + 17], gsc[:, k, 16:17], modT[:, k, 16:17],
                       ALU.mult, ALU.add, [ps, gsc, modT], [hT], waw=False)

    def make_rwkv():
        NF = TO + TS
        raw = fw.sb("raw", [128, NF + 1]); tmpA = fw.sb("tmpA", [128, NF]); tmpB = fw.sb("tmpB", [128, NF]); tmpC = fw.sb("tmpC", [128, NF])
        SETS = []
        for si in range(2):
            SETS.append((fw.sb("AR", [128, 2, NF]), fw.sb("BKt", [128, 2, NF]), fw.sb("vT", [128, NF]), fw.sb("eL", [128, NF]),
                         fw.sb("lw", [128, NF]), fw.sb("bonus", [128, NF]), fw.sb("zbS", [128, NF], BF16), fw.sb("ys16", [128, 16])))
        Lc = fw.sb("Lc", [128, NF]); YT = fw.sb("YT", [128, NF])
        ws16 = fw.sb("ws16", [128, 16]); st1b = fw.sb("st1b", [128, TS, 64], BF16)
        tA2 = fw.sb("tA2", [128, NF]); tB2 = fw.sb("tB2", [128, NF]); tC2 = fw.sb("tC2", [128, NF])
        l2ring = [fw.sb(f"l2r{i}", [96, 128], BF16) for i in range(6)]
        l2pos = [0]

        def load_small(src, hp):
            t_ = l2ring[l2pos[0] % len(l2ring)]; l2pos[0] += 1
            fw.dma("pool", t_[:, :], src[0:96, hp * 128:(hp + 1) * 128], reads=[src], writes=[t_])
            return t_
        rmask = fw.sb("rmask", [128, TO])
        fw.op("dve", lambda: V.memset(rmask[:], 1.0), writes=[rmask])
        fw.op("dve", lambda: V.memset(rmask[:].rearrange("p (c t) -> p c t", t=NCH)[:, :, 0:1], 0.0), writes=[rmask])
        thw = fw.sb("thw", [96, NF], BF16); adx = fw.sb("adx", [96, NF], BF16)
        Hst = fw.sb("Hst", [128, 64])
        tk8 = fw.sb("tk8", [128, 8, 5, 64]); sc8 = fw.sb("sc8", [128, 8, 5, 64]); qp8 = fw.sb("qp8", [128, 8, 3, 64])
        AW8 = fw.sb("AW8", [128, 8, 2, 64]); U08 = fw.sb("U08", [128, 8]); GT8 = fw.sb("GT8", [128, 8, 64])
        ZM8 = fw.sb("ZM8", [128, 8, 2, 64]); Qy8 = fw.sb("Qy8", [128, 8, 64]); H08 = fw.sb("H08", [128, 9, 64])
        shp_sb = fw.sb("shp_sb", [128, 50]); shs_sb = fw.sb("shs_sb", [128, 50, TS])
        fw.op("dve", lambda: V.memset(shp_sb[:], 0.0), writes=[shp_sb])
        fw.op("dve", lambda: V.memset(shs_sb[:], 0.0), writes=[shs_sb])
        sst = fw.sb("sst", [TS, 128])
        Hs = fw.sb("Hs", [128, TS, 64]); st1 = fw.sb("st1", [128, TS, 64]); st2 = fw.sb("st2", [128, TS, 64])

        def shifted(ps3, n, ns, mu_ap, obuf, out_ap_fn, tile_idx, col_c0, with_out, mfac, nparts=128):
            P = slice(0, nparts)
            tot = 1 + n + ns
            for bi, (a, b) in enumerate(blocks(0, tot)):
                cp("act", raw[P, a:b], ps3[bi][P, 0:b - a], [ps3[bi]], [raw], waw=(bi == 0))
            ts("dve", raw[P, 0:1], raw[P, 0:1], mfac[P, 0:1], None, ALU.mult, None, [raw, mfac], [raw])
            yield
            if with_out:
                cp("act", shp_sb[P, tile_idx:tile_idx + 1], raw[P, n:n + 1], [raw], [shp_sb], waw=False)
                cp("act", shs_sb[P, tile_idx, :], raw[P, n + 1:n + 1 + ns], [raw], [shs_sb], waw=False)
            tt("dve", tmpA[P, 0:n], raw[P, 0:n], raw[P, 1:n + 1], ALU.subtract, [raw], [tmpA])
            yield
            stt(out_ap_fn(0, n), tmpA[P, 0:n], mu_ap, raw[P, 1:n + 1], ALU.mult, ALU.add, [tmpA, raw, pc], [obuf], waw=False)
            yield
            if ns:
                fw.dma("sp", sst[0:ns, 0:nparts], sshift[:, col_c0 - C_R:col_c0 - C_R + nparts], reads=[sshift], writes=[sst])
                fw.op("pe", lambda: T.transpose(WK[0][P, 0:ns], sst[0:ns, 0:nparts], ident[0:ns, 0:ns]),
                      reads=[sst, ident], writes=[WK[0]])
                tt("dve", tmpA[P, n:n + ns], WK[0][P, 0:ns], raw[P, n + 1:n + 1 + ns], ALU.subtract, [WK[0], raw], [tmpA])
                stt(out_ap_fn(n, n + ns), tmpA[P, n:n + ns], mu_ap, raw[P, n + 1:n + 1 + ns], ALU.mult, ALU.add, [tmpA, raw, pc], [obuf], waw=False)
            yield

        def rwkv_phase(hT, n, ns, with_out, mfac, hfac, is_last):
            nf = n + ns
            tot = 1 + nf
            for wkb in WK[0:4]:
                fw.op("dve", lambda wkb=wkb: V.memset(wkb[:], 0.0), writes=[wkb])
            for li, (c0, mucol, dst, func) in enumerate(((C_WD, PC_MUWD, thw, AF.Tanh), (C_AD, PC_MUAD, adx, AF.Identity))):
                wt = load_w(w_in, c0, 96)
                for bi, (a, b) in enumerate(blocks(0, tot)):
                    for k in range(16):
                        mm(PJ[bi][0:96, 0:b - a], wt[:, k, 0:96], hT[:, k, a:b], k == 0, k == 15, [wt, hT], [PJ[bi]])
                for _ in shifted(PJ, n, ns, pc[0:96, mucol:mucol + 1], tmpB, lambda a, b: tmpB[0:96, a:b], 48 + li, c0, is_last, mfac, nparts=96):
                    pass
                act(dst[:, 0:nf], tmpB[0:96, 0:nf], func, [tmpB], [dst])
            def F(hp):
                AR, BKt, vT, eL, lw, bonus, zbS, ys16 = SETS[hp % 2]
                w2b = load_small(w2, hp); a2b = load_small(a2, hp)
                wr = load_w(w_in, C_R + 128 * hp) if with_out else None
                wk = load_w(w_in, C_K + 128 * hp)
                wv = load_w(w_in, C_V + 128 * hp)
                wzb = load_w(w_in, C_ZB + 128 * hp) if with_out else None
                col = slice(hp, hp + 1)
                yield
                if with_out:
                    inproj(PJ, wr, 0, hT, 0, tot)
                    yield from shifted(PJ, n, ns, pc[:, PC_MUR + hp:PC_MUR + hp + 1], AR, lambda a, b: AR[:, 1, a:b], hp, C_R + 128 * hp, is_last, mfac)
                yield
                inproj(PJ, wk, 0, hT, 0, tot)
                yield from shifted(PJ, n, ns, pc[:, PC_MUK + hp:PC_MUK + hp + 1], BKt, lambda a, b: BKt[:, 1, a:b], 16 + hp, C_K + 128 * hp, is_last, mfac)
                yield
                inproj(PJ, wv, 0, hT, 0, tot)
                yield from shifted(PJ, n, ns, pc[:, PC_MUV + hp:PC_MUV + hp + 1], vT, lambda a, b: vT[:, a:b], 32 + hp, C_V + 128 * hp, is_last, mfac)
                yield
                for bi, (a, b) in enumerate(blocks(0, nf)):
                    mm(PJ[bi][:, 0:b - a], w2b[:, :], thw[:, a:b], True, True, [w2b, thw], [PJ[bi]])
                    act(lw[:, a:b], PJ[bi][:, 0:b - a], AF.Sigmoid, [PJ[bi], pc], [lw], bias=pc[:, PC_W0 + hp:PC_W0 + hp + 1], waw=(bi == 0))
                yield
                for bi, (a, b) in enumerate(blocks(0, nf)):
                    mm(PJ[bi][:, 0:b - a], a2b[:, :], adx[:, a:b], True, True, [a2b, adx], [PJ[bi]])
                    act(tmpB[:, a:b], PJ[bi][:, 0:b - a], AF.Sigmoid, [PJ[bi], pc], [tmpB], bias=pc[:, PC_A0 + hp:PC_A0 + hp + 1], waw=(bi == 0))
                if with_out:
                    inproj(PJ, wzb, 0, hT, 1, tot)
                    for bi, (a, b) in enumerate(blocks(0, nf)):
                        act(zbS[:, a:b], PJ[bi][:, 0:b - a], AF.Silu, [PJ[bi]], [zbS], waw=(bi == 0))
                yield
                ts("dve", AR[:, 0, 0:nf], BKt[:, 1, 0:nf], pc[:, PC_KK + hp:PC_KK + hp + 1], None, ALU.mult, None, [BKt, pc], [AR], waw=False)
                yield
                tt("dve", tmpA[:, 0:nf], AR[:, 0, 0:nf], AR[:, 0, 0:nf], ALU.mult, [AR], [tmpA])
                yield
                for bi, (a, b) in enumerate(blocks(0, nf)):
                    mm(PJ[bi][:, 0:b - a], bones[:, :], tmpA[:, a:b], True, True, [bones, tmpA], [PJ[bi]])
                    ts("dve", tmpC[:, a:b], PJ[bi][:, 0:b - a], 1e-19, None, ALU.max, None, [PJ[bi]], [tmpC], waw=(bi == 0))
                act(tmpC[:, 0:nf], tmpC[:, 0:nf], AF.Ln, [tmpC], [tmpC])
                yield
                act(tmpC[:, 0:nf], tmpC[:, 0:nf], AF.Exp, [tmpC], [tmpC], scale=-0.5)
                yield
                tt("dve", AR[:, 0, 0:nf], AR[:, 0, 0:nf], tmpC[:, 0:nf], ALU.mult, [AR, tmpC], [AR], waw=False)
                yield
                yield
                ts("dve", tmpA[:, 0:nf], tmpB[:, 0:nf], 1.0, pc[:, PC_KA + hp:PC_KA + hp + 1], ALU.subtract, ALU.mult, [tmpB, pc], [tmpA])
                yield
                stt(BKt[:, 1, 0:nf], tmpA[:, 0:nf], 1.0, BKt[:, 1, 0:nf], ALU.add, ALU.mult, [tmpA, BKt], [BKt], waw=False)
                yield
                yield
                tt("dve", BKt[:, 0, 0:nf], AR[:, 0, 0:nf], tmpB[:, 0:nf], ALU.mult, [AR, tmpB], [BKt], waw=False)
                yield
                if with_out:
                    stt(tmpA[:, 0:nf], AR[:, 1, 0:nf], pc[:, PC_RK + hp:PC_RK + hp + 1], BKt[:, 1, 0:nf], ALU.mult, ALU.mult, [AR, BKt, pc], [tmpA])
                    for bi, (a, b) in enumerate(blocks(0, nf)):
                        mm(PJ[bi][:, 0:b - a], bones[:, :], tmpA[:, a:b], True, True, [bones, tmpA], [PJ[bi]])
                        tt("dve", bonus[:, a:b], PJ[bi][:, 0:b - a], vT[:, a:b], ALU.mult, [PJ[bi], vT], [bonus], waw=(bi == 0))
                yield
                fw.op("dve", lambda: V.tensor_tensor_scan(Lc[:, 0:n], rmask[:, 0:n], lw[:, 0:n], 0.0, ALU.mult, ALU.add), reads=[rmask, lw], writes=[Lc])
                yield
                act(eL[:, 0:n], Lc[:, 0:n], AF.Exp, [Lc], [eL], scale=-EXPM05)
                yield
                tt("dve", tmpA[:, 0:n], Lc[:, 0:n], lw[:, 0:n], ALU.subtract, [Lc, lw], [tmpA])
                yield
                act(tmpA[:, 0:n], tmpA[:, 0:n], AF.Exp, [tmpA], [tmpA], scale=-EXPM05)
                yield
                act(tmpC[:, 0:n], Lc[:, 0:n], AF.Exp, [Lc], [tmpC], scale=EXPM05)
                yield
                if with_out:
                    tt("dve", AR[:, 1, 0:n], AR[:, 1, 0:n], eL[:, 0:n], ALU.mult, [AR, eL], [AR], waw=False)
                stt(AR[:, 0, 0:n], AR[:, 0, 0:n], -1.0, tmpA[:, 0:n], ALU.mult, ALU.mult, [AR, tmpA], [AR], waw=False)
                yield
                tt("dve", BKt[:, 0, 0:n], BKt[:, 0, 0:n], tmpC[:, 0:n], ALU.mult, [BKt, tmpC], [BKt], waw=False)
                yield
                tt("dve", BKt[:, 1, 0:n], BKt[:, 1, 0:n], tmpC[:, 0:n], ALU.mult, [BKt, tmpC], [BKt], waw=False)
                yield
                if with_out and ns:
                    so = slice(n, n + ns)
                    fw.dma("sp", Hs[:], swkv[hp], reads=[swkv], writes=[Hs])
                    act(ws16[:, 0:ns], lw[:, so], AF.Exp, [lw], [ws16], scale=-EXPM05)
                    bc = lambda ap: ap.unsqueeze(2).to_broadcast([128, ns, 64])
                    flat = lambda b_: b_[:].rearrange("p s i -> p (s i)")
                    def bsum_ev(src, ones_, other_fn, other_bufs):
                        for hb in range(2):
                            mm(WK[4][:, :], ones_[:, :], flat(src)[:, hb * 512:(hb + 1) * 512], True, True, [ones_, src], [WK[4]])
                            tt("dve", hv(st2, hb), WK[4][:, :].rearrange("p (s i) -> p s i", i=64), other_fn(hb), ALU.mult, [WK[4]] + other_bufs, [st2], waw=(hb == 0))
                    hv = lambda b_, hb: b_[:, hb * 8:(hb + 1) * 8, :]
                    bc8 = lambda ap, hb: ap[:, hb * 8:(hb + 1) * 8].unsqueeze(2).to_broadcast([128, 8, 64])
                    kk_s, r_s, b_s_, kf_s, v_s, w_s = AR[:, 0, so], AR[:, 1, so], BKt[:, 0, so], BKt[:, 1, so], vT[:, so], ws16[:, 0:ns]
                    tt("dve", st1b[:], Hs[:], bc(kk_s), ALU.mult, [Hs, AR], [st1b])
                    yield
                    bsum_ev(st1b, bones_bf, lambda hb: bc8(b_s_, hb), [BKt])
                    yield
                    tt("dve", Hs[:], Hs[:], bc(w_s), ALU.mult, [Hs, ws16], [Hs])
                    tt("dve", Hs[:], Hs[:], st2[:], ALU.subtract, [Hs, st2], [Hs])
                    tt("dve", st1[:], ident2[:].unsqueeze(1).to_broadcast([128, ns, 64]), bc(v_s), ALU.mult, [ident2, vT], [st1])
                    yield
                    bsum_ev(st1, bones, lambda hb: bc8(kf_s, hb), [BKt])
                    yield
                    tt("dve", Hs[:], Hs[:], st2[:], ALU.add, [Hs, st2], [Hs])
                    fw.dma("sp", wkv_s[hp], Hs[:], reads=[Hs], writes=[wkv_s], waw=False)
                    tt("dve", st1b[:], Hs[:], bc(r_s), ALU.mult, [Hs, AR], [st1b])
                    yield
                    bsum_ev(st1b, bones_bf, lambda hb: ident2[:].unsqueeze(1).to_broadcast([128, 8, 64]), [ident2])
                    yield
                    fw.op("dve", lambda: V.tensor_reduce(ys16[:, 0:ns], st2[:], AX.X, ALU.add), reads=[st2], writes=[ys16])
                    yield
            def CT(hp):
                AR, BKt, vT, eL, lw, bonus, zbS, ys16 = SETS[hp % 2]
                tmpA, tmpB, tmpC = tA2, tB2, tC2
                NC8 = n // NCH
                tkC = [Buf(tk8.t) for _ in range(8)]; scC = [Buf(sc8.t) for _ in range(8)]; qpC = [Buf(qp8.t) for _ in range(8)]
                awC = [Buf(AW8.t) for _ in range(8)]; u0C = [Buf(U08.t) for _ in range(8)]; gtC = [Buf(GT8.t) for _ in range(8)]
                zmC = [Buf(ZM8.t) for _ in range(8)]; qyC = [Buf(Qy8.t) for _ in range(8)]; h0C = [Buf(H08.t) for _ in range(9)]
                for lst, par in ((tkC, tk8), (scC, sc8), (qpC, qp8), (awC, AW8), (u0C, U08), (gtC, GT8), (zmC, ZM8), (qyC, Qy8), (h0C, H08)):
                    for bb in lst:
                        bb.w = dict(par.w); bb.r = dict(par.r)
                bankT = [WK[0], WK[1], WK[2], WK[3]]
                ts("dve", H08[:, 0, :], Hcarry[:, hp, :], hfac[:, 0:1], None, ALU.mult, None, [Hcarry, hfac], [h0C[0]])
                srcs = ((vT, lambda Pp: vT[Pp, 0:n]), (BKt, lambda Pp: BKt[Pp, 0, 0:n]), (BKt, lambda Pp: BKt[Pp, 1, 0:n]), (AR, lambda Pp: AR[Pp, 0, 0:n]))
                for xi, (sbuf_, sfn) in enumerate(srcs):
                    for hh in range(2):
                        for bi_ in range(2):
                            for bj_ in range(2):
                                Pin = slice(64 * hh + 32 * bi_, 64 * hh + 32 * bi_ + 32)
                                Pout = slice(64 * hh + 32 * bj_, 64 * hh + 32 * bj_ + 32)
                                in_ap = sfn(Pin).rearrange("p (c t) -> p c t", t=NCH)[:, :, 32 * bj_:32 * bj_ + 32]
                                out_ap = tk8[Pout, :, xi, 32 * bi_:32 * bi_ + 32]
                                fw.op("dve", lambda in_ap=in_ap, out_ap=out_ap: V.transpose(out_ap, in_ap), reads=[sbuf_], writes=tkC, waw=False)
                    yield
                for c in range(NC8):
                    cs = slice(c * NCH, (c + 1) * NCH)
                    ps = bankT[c % 4]
                    first = True
                    for hh in range(2):
                        Pq = slice(64 * hh, 64 * hh + 64)
                        if with_out:
                            rhsA = AR[Pq, :, cs]; wA = 128
                        else:
                            rhsA = AR[Pq, 0, cs]; wA = 64
                        fw.op("pe", lambda Pq=Pq, rhsA=rhsA, wA=wA, ps=ps, cs=cs: T.matmul(ps[Pq, 0:wA], BKt[Pq, 0, cs], rhsA, start=True, stop=True),
                              reads=[AR, BKt], writes=[ps], waw=first, inc=False)
                        first = False
                        fw.op("pe", lambda Pq=Pq, rhsA=rhsA, wA=wA, ps=ps, cs=cs: T.matmul(ps[Pq, 128:128 + wA], BKt[Pq, 1, cs], rhsA, start=True, stop=True),
                              reads=[AR, BKt], writes=[ps], waw=False, inc=False)
                        fw.op("pe", lambda Pq=Pq, ps=ps, cs=cs: T.matmul(ps[Pq, 256:320], AR[Pq, 0, cs], BKt[Pq, 0, cs], start=True, stop=True),
                              reads=[AR, BKt], writes=[ps], waw=False, inc=(hh == 1))
                    tt("dve", sc8[:, c, :, :], ps[:, 0:320].rearrange("p (x f) -> p x f", f=64), mask5[:], ALU.mult, [ps, mask5], [scC[c]], waw=False)
                    yield
                cp("act", qp8[:, :, 1, :], sc8[:, :, 0, :], scC, qpC, waw=False)
                cp("act", qp8[:, :, 2, :], sc8[:, :, 4, :], scC, qpC, waw=False)
                tt("dve", qp8[:, :, 0, :], sc8[:, :, 0, :], ident2[:].unsqueeze(1).to_broadcast([128, NC8, 64]), ALU.add, scC + [ident2], qpC, waw=False)
                for rd in range(6):
                    for bq in range(NC8 // 2):
                        ps = bankT[bq]
                        first = True
                        for ci in range(2):
                            c = 2 * bq + ci
                            o0 = ci * 192
                            for hh in range(2):
                                Pq = slice(64 * hh, 64 * hh + 64)
                                last = (ci == 1 and hh == 1)
                                if rd == 0:
                                    fw.op("pe", lambda Pq=Pq, c=c, o0=o0, ps=ps: T.matmul(ps[Pq, o0 + 64:o0 + 128], qp8[Pq, c, 2, :], qp8[Pq, c, 1, :], start=True, stop=True),
                                          reads=[qpC[c]], writes=[ps], waw=first, inc=False)
                                elif rd < 5:
                                    fw.op("pe", lambda Pq=Pq, c=c, o0=o0, ps=ps: T.matmul(ps[Pq, o0:o0 + 128], qp8[Pq, c, 2, :], qp8[Pq, c, 0:2, :], start=True, stop=True),
                                          reads=[qpC[c]], writes=[ps], waw=first, inc=False)
                                else:
                                    fw.op("pe", lambda Pq=Pq, c=c, o0=o0, ps=ps: T.matmul(ps[Pq, o0:o0 + 64], qp8[Pq, c, 2, :], qp8[Pq, c, 0, :], start=True, stop=True),
                                          reads=[qpC[c]], writes=[ps], waw=first, inc=last)
                                first = False
                                if rd < 5:
                                    fw.op("pe", lambda Pq=Pq, c=c, o0=o0, ps=ps: T.matmul(ps[Pq, o0 + 128:o0 + 192], qp8[Pq, c, 1, :], qp8[Pq, c, 2, :], start=True, stop=True),
                                          reads=[qpC[c]], writes=[ps], waw=False, inc=last)
                        yield
                        pv = ps[:, 0:384].rearrange("p (c x f) -> p c x f", c=2, x=3)
                        if rd >= 1:
                            tt("dve", qp8[:, 2 * bq:2 * bq + 2, 0, :], qp8[:, 2 * bq:2 * bq + 2, 0, :], pv[:, :, 0, :], ALU.add, [qpC[2 * bq], qpC[2 * bq + 1], ps], [qpC[2 * bq], qpC[2 * bq + 1]], waw=False)
                        if rd < 5:
                            cp("act", qp8[:, 2 * bq:2 * bq + 2, 1:3, :], pv[:, :, 1:3, :], [ps], [qpC[2 * bq], qpC[2 * bq + 1]], waw=False)
                ps = bankT[0]
                first = True
                for c in range(NC8):
                    for hh in range(2):
                        Pq = slice(64 * hh, 64 * hh + 64)
                        last = (c == NC8 - 1 and hh == 1)
                        fw.op("pe", lambda Pq=Pq, c=c, ps=ps: T.matmul(ps[Pq, c * 64:(c + 1) * 64], sc8[Pq, c, 2, :], tk8[Pq, c, 0, :], start=True, stop=True),
                              reads=[scC[c], tkC[c]], writes=[ps], waw=first, inc=last)
                        first = False
                cp("act", tk8[:, :, 4, :], ps[:, :].rearrange("p (c f) -> p c f", f=64), [ps], tkC, waw=False)
                yield
                for bq in range(NC8 // 4):
                    ps = bankT[2 + bq]
                    first = True
                    for ci in range(4):
                        c = 4 * bq + ci
                        o0 = ci * 128
                        for hh in range(2):
                            Pq = slice(64 * hh, 64 * hh + 64)
                            last = (ci == 3 and hh == 1)
                            fw.op("pe", lambda Pq=Pq, c=c, o0=o0, ps=ps: T.matmul(ps[Pq, o0:o0 + 128], qp8[Pq, c, 0, :], tk8[Pq, c, 3:5, :], start=True, stop=True),
                                  reads=[qpC[c], tkC[c]], writes=[ps], waw=first, inc=last)
                            first = False
                    cp("act", AW8[:, 4 * bq:4 * bq + 4, :, :], ps[:, :].rearrange("p (c x f) -> p c x f", c=4, x=2), [ps], awC[4 * bq:4 * bq + 4], waw=False)
                    yield
                for bq in range(NC8 // 2):
                    ps = bankT[bq]
                    first = True
                    for ci in range(2):
                        c = 2 * bq + ci
                        cs = slice(c * NCH, (c + 1) * NCH)
                        o0 = ci * 256
                        for hh in range(2):
                            Pq = slice(64 * hh, 64 * hh + 64)
                            last = (ci == 1 and hh == 1)
                            fw.op("pe", lambda Pq=Pq, c=c, o0=o0, ps=ps: T.matmul(ps[Pq, o0:o0 + 64], tk8[Pq, c, 1, :], AW8[Pq, c, 1, :], start=True, stop=False),
                                  reads=[tkC[c], awC[c]], writes=[ps], waw=first, inc=False)
                            first = False
                            fw.op("pe", lambda Pq=Pq, c=c, o0=o0, ps=ps: T.matmul(ps[Pq, o0:o0 + 64], tk8[Pq, c, 2, :], tk8[Pq, c, 0, :], start=False, stop=True),
                                  reads=[tkC[c]], writes=[ps], waw=False, inc=False)
                            fw.op("pe", lambda Pq=Pq, c=c, o0=o0, ps=ps: T.matmul(ps[Pq, o0 + 192:o0 + 256], AW8[Pq, c, 0, :], tk8[Pq, c, 1, :], start=True, stop=True),
                                  reads=[awC[c], tkC[c]], writes=[ps], waw=False, inc=(last and not with_out))
                            if with_out:
                                fw.op("pe", lambda Pq=Pq, c=c, o0=o0, ps=ps: T.matmul(ps[Pq, o0 + 64:o0 + 128], AW8[Pq, c, 1, :], sc8[Pq, c, 1, :], start=True, stop=False),
                                      reads=[awC[c], scC[c]], writes=[ps], waw=False, inc=False)
                                fw.op("pe", lambda Pq=Pq, c=c, o0=o0, ps=ps: T.matmul(ps[Pq, o0 + 64:o0 + 128], tk8[Pq, c, 0, :], sc8[Pq, c, 3, :], start=False, stop=True),
                                      reads=[tkC[c], scC[c]], writes=[ps], waw=False, inc=False)
                                fw.op("pe", lambda Pq=Pq, c=c, o0=o0, ps=ps: T.matmul(ps[Pq, o0 + 128:o0 + 192], AW8[Pq, c, 0, :], sc8[Pq, c, 1, :], start=True, stop=True),
                                      reads=[awC[c], scC[c]], writes=[ps], waw=False, inc=last)
                    pv = ps[:, 0:512].rearrange("p (c x f) -> p c x f", c=2, x=4)
                    yield
                    c2 = slice(2 * bq, 2 * bq + 2)
                    ecol2 = eL[:, 2 * bq * NCH:(2 * bq + 2) * NCH].rearrange("p (c t) -> p c t", t=NCH)[:, :, NCH - 1:NCH].to_broadcast([128, 2, 64])
                    tt("dve", ZM8[:, c2, 0, :], pv[:, :, 0, :], ecol2, ALU.mult, [ps, eL], [zmC[2 * bq], zmC[2 * bq + 1]], waw=False)
                    tt("dve", GT8[:, c2, :], pv[:, :, 3, :], ident2[:].unsqueeze(1).to_broadcast([128, 2, 64]), ALU.add, [ps, ident2], [gtC[2 * bq], gtC[2 * bq + 1]], waw=False)
                    if with_out:
                        cp("act", ZM8[:, c2, 1, :], pv[:, :, 1, :], [ps], [zmC[2 * bq], zmC[2 * bq + 1]], waw=False)
                        tt("dve", Qy8[:, c2, :], pv[:, :, 2, :], AR[:, 1, 2 * bq * NCH:(2 * bq + 2) * NCH].rearrange("p (c t) -> p c t", t=NCH), ALU.add, [ps, AR], [qyC[2 * bq], qyC[2 * bq + 1]], waw=False)
                for c in range(NC8):
                    for hh in range(2):
                        Pq = slice(64 * hh, 64 * hh + 64)
                        fw.op("pe", lambda Pq=Pq, c=c: T.matmul(WK[4][Pq, c * 64:(c + 1) * 64], GT8[Pq, c, :], H08[Pq, c, :], start=True, stop=True),
                              reads=[gtC[c], h0C[c]], writes=[WK[4]], waw=(hh == 0 and c == 0), inc=(hh == 1))
                    ecol = eL[:, c * NCH + NCH - 1:c * NCH + NCH]
                    stt(H08[:, c + 1, :], WK[4][:, c * 64:(c + 1) * 64], ecol, ZM8[:, c, 0, :], ALU.mult, ALU.add, [WK[4], eL, zmC[c]], [h0C[c + 1]], waw=False)
                    yield
                if with_out:
                    for c in range(NC8):
                        for hh in range(2):
                            Pq = slice(64 * hh, 64 * hh + 64)
                            fw.op("pe", lambda Pq=Pq, c=c: T.matmul(WK[3][Pq, c * 64:(c + 1) * 64], H08[Pq, c, :], Qy8[Pq, c, :], start=True, stop=True),
                                  reads=[h0C[c], qyC[c]], writes=[WK[3]], waw=(hh == 0 and c == 0), inc=(hh == 1 and c == NC8 - 1))
                    tt("dve", YT[:, 0:n].rearrange("p (c t) -> p c t", t=NCH), WK[3][:, :].rearrange("p (c t) -> p c t", t=NCH), ZM8[:, :, 1, :], ALU.add, [WK[3]] + zmC, [YT], waw=False)
                for lst, par in ((tkC, tk8), (scC, sc8), (qpC, qp8), (awC, AW8), (u0C, U08), (gtC, GT8), (zmC, ZM8), (qyC, Qy8), (h0C, H08)):
                    for bb in lst:
                        for kx, vx in bb.w.items():
                            if par.w.get(kx, 0) < vx:
                                par.w[kx] = vx
                        for kx, vx in bb.r.items():
                            if par.r.get(kx, 0) < vx:
                                par.r[kx] = vx
                cp("act", Hst[:], H08[:, NC8, :], [H08], [Hst])
                cp("act", Hcarry[:, hp, :], Hst[:], [Hst], [Hcarry], waw=False)
                if not with_out:
                    return
                if is_last:
                    fw.dma("sp", wkv_p[hp], Hcarry[:, hp, :], reads=[Hcarry], writes=[wkv_p], waw=False)
                yield
                if ns:
                    cp("dve", YT[:, n:n + ns], ys16[:, 0:ns], [ys16], [YT], waw=False)
                for bi, (a, b) in enumerate(blocks(0, nf)):
                    mm(PJ[bi][:, 0:b - a], bones64[:, :], YT[:, a:b], True, True, [bones64, YT], [PJ[bi]])
                    tt("dve", tmpA[:, a:b], YT[:, a:b], PJ[bi][:, 0:b - a], ALU.subtract, [YT, PJ[bi]], [tmpA], waw=(bi == 0))
                tt("dve", tmpB[:, 0:nf], tmpA[:, 0:nf], tmpA[:, 0:nf], ALU.mult, [tmpA], [tmpB])
                for bi, (a, b) in enumerate(blocks(0, nf)):
                    mm(PJ[bi][:, 0:b - a], bones64[:, :], tmpB[:, a:b], True, True, [bones64, tmpB], [PJ[bi]])
                    act(tmpC[:, a:b], PJ[bi][:, 0:b - a], AF.Ln, [PJ[bi]], [tmpC], bias=gneps[:, 0:1], waw=(bi == 0))
                act(tmpC[:, 0:nf], tmpC[:, 0:nf], AF.Exp, [tmpC], [tmpC], scale=-0.5)
                tt("dve", tmpA[:, 0:nf], tmpA[:, 0:nf], tmpC[:, 0:nf], ALU.mult, [tmpA, tmpC], [tmpA])
                ts("dve", tmpA[:, 0:nf], tmpA[:, 0:nf], pc[:, PC_GNG + hp:PC_GNG + hp + 1], pc[:, PC_GNB + hp:PC_GNB + hp + 1], ALU.mult, ALU.add, [tmpA, pc], [tmpA])
                tt("dve", tmpA[:, 0:nf], tmpA[:, 0:nf], bonus[:, 0:nf], ALU.add, [tmpA, bonus], [tmpA])
                tt("dve", yb_T[:, hp, 0:nf], tmpA[:, 0:nf], zbS[:, 0:nf], ALU.mult, [tmpA, zbS], [yb_T], waw=False)

            def interleave(g1, g2):
                gens = [g1, g2]
                while gens:
                    for g in list(gens):
                        try:
                            next(g)
                        except StopIteration:
                            gens.remove(g)
            for _ in F(0):
                pass
            for hp in range(16):
                g1 = CT(hp)
                g2 = F(hp + 1) if hp < 15 else iter(())
                interleave(g1, g2)
        def fin():
            fw.dma("sp", sh_p[:], shp_sb[:], reads=[shp_sb], writes=[sh_p])
            fw.dma("sp", sh_s[:], shs_sb[:], reads=[shs_sb], writes=[sh_s])
        return rwkv_phase, fin
    hT = fw.sb("hT", [128, 16, 1 + NF], BF16)
    yb_T = fw.sb("yb_T", [128, 16, NF], BF16)
    Hcarry = fw.sb("Hcarry", [128, 16, 64])
    fw.op("pool", lambda: G.memset(Hcarry[:], 0.0), writes=[Hcarry])
    fw.op("dve", lambda: V.memset(hT[:, :, 0:1], 0.0), writes=[hT])

    def gmo_segment(seg, ns):
        nf = TO + ns
        nch = 4 + (1 if ns else 0)
        bank5 = [PJ[0], PJ[1], PJ[2], WK[0], WK[1]]
        tA = fw.sb("tA", [128, NF]); tB = fw.sb("tB", [128, NF]); tC = fw.sb("tC", [128, NF])
        vn = fw.sb("vn", [128, 5, 1024], BF16)
        ya_T = fw.sb("ya_T", [128, 8, NF], BF16)
        mg_T = fw.sb("mg_T", [128, 16, NF], BF16)
        mG = fw.mark()
        lng = fw.sb("lng", [128, 1024]); lnb = fw.sb("lnb", [128, 1024]); bsb = fw.sb("bsb", [128, 1024])
        fw.dma("sp", lng[:], lnv_g[0:1, :].to_broadcast([128, 1024]), reads=[lnv_g], writes=[lng])
        fw.dma("sp", lnb[:], lnv_b[0:1, :].to_broadcast([128, 1024]), reads=[lnv_b], writes=[lnb])
        fw.dma("sp", bsb[:], b_s[0:1, :].to_broadcast([128, 1024]), reads=[b_s], writes=[bsb])
        wsc = fw.sb("wsc", [16, 16])
        fw.dma("sp", wsc[:, 0:8], ws00[0:1, :].to_broadcast([16, 8]), reads=[ws00], writes=[wsc])
        fw.dma("sp", wsc[:, 8:16], bs0[0:1, :].to_broadcast([16, 8]), reads=[bs0], writes=[wsc], waw=False)
        wm = fw.sb("wm", [128, 8, 128], BF16)
        gtmp = fw.sb("gtmp", [128, 1024]); gtmp2 = fw.sb("gtmp2", [128, 1024]); stats = fw.sb("stats", [128, 16])
        vg5 = [fw.sb(f"vg5_{i}", [128, 1024]) for i in range(nch)]
        fw.dma("sp", gtmp[:, :].rearrange("p (g t) -> p g t", t=128), w_sT[:].rearrange("g s t -> s g t"), reads=[w_sT], writes=[gtmp])
        tt("dve", wm[:], gtmp[:, :].rearrange("p (g t) -> p g t", t=128), triu[:].unsqueeze(1).to_broadcast([128, 8, 128]), ALU.mult, [gtmp, triu], [wm])
        mixs = fw.sb("mixs", [16, 1024]); mixsT = fw.sb("mixsT", [128, 8, 16])

        def gelu_from(src_ap, n, width, R, outbuf, out_ap):
            P = slice(0, n)
            act(gtmp[P, 0:width], src_ap, AF.Square, R, [gtmp])
            ts("dve", gtmp[P, 0:width], gtmp[P, 0:width], 0.044715, 1.0, ALU.mult, ALU.add, [gtmp], [gtmp])
            tt("dve", gtmp[P, 0:width], gtmp[P, 0:width], src_ap, ALU.mult, R + [gtmp], [gtmp])
            act(gtmp[P, 0:width], gtmp[P, 0:width], AF.Sigmoid, [gtmp], [gtmp], scale=1.5957691216057308)
            tt("dve", out_ap, gtmp[P, 0:width], src_ap, ALU.mult, R + [gtmp], [outbuf], waw=False)

        for cb in range(2):
            for q in range(4):
                wt = load_w(w_in, 1024 + cb * 512 + q * 128)
                for ci in range(nch):
                    n = 128 if ci < 4 else 16
                    c0 = 1 + 128 * ci
                    for k in range(16):
                        mm(bank5[ci][0:n, q * 128:(q + 1) * 128], hT[:, k, c0:c0 + n], wt[:, k, :], k == 0, k == 15, [hT, wt], [bank5[ci]])
            for ci in range(nch):
                n = 128 if ci < 4 else 16
                gelu_from(bank5[ci][0:n, :], n, 512, [bank5[ci]], vg5[ci], vg5[ci][0:n, cb * 512:(cb + 1) * 512])
        for ci in range(nch):
            n = 128 if ci < 4 else 16
            P = slice(0, n)
            g2 = vg5[ci]
            for cb in range(2):
                fw.op("dve", lambda cb=cb: V.bn_stats(stats[P, cb * 6:(cb + 1) * 6], g2[P, cb * 512:(cb + 1) * 512]), reads=[g2], writes=[stats], waw=False)
            fw.op("dve", lambda: V.bn_aggr(stats[P, 12:14], stats[P, 0:12]), reads=[stats], writes=[stats], waw=False)
            act(stats[P, 14:15], stats[P, 13:14], AF.Sqrt, [stats, neps], [stats], bias=neps[P, 0:1], waw=False)
            fw.op("dve", lambda: V.reciprocal(stats[P, 15:16], stats[P, 14:15]), reads=[stats], writes=[stats], waw=False)
            ts("dve", g2[P, :], g2[P, :], stats[P, 12:13], stats[P, 15:16], ALU.subtract, ALU.mult, [g2, stats], [g2])
            tt("dve", g2[P, :], g2[P, :], lng[P, :], ALU.mult, [g2, lng], [g2])
            if ci < 4:
                tt("dve", vn[:, ci, :], g2[:, :], lnb[:, :], ALU.add, [g2, lnb], [vn], waw=False)
            else:
                tt("dve", g2[P, :], g2[P, :], lnb[P, :], ALU.add, [g2, lnb], [g2])
                fw.dma("sp", cv_s[:], g2[0:16, :], reads=[g2], writes=[cv_s])
                g3 = lambda ap: ap.rearrange("p (g c) -> p g c", c=128)
                tt("dve", g3(mixs[:, :]), g3(g2[0:16, :]), wsc[:, 0:8].unsqueeze(2).to_broadcast([16, 8, 128]), ALU.mult, [g2, wsc], [mixs])
                tt("dve", g3(mixs[:, :]), g3(mixs[:, :]), wsc[:, 8:16].unsqueeze(2).to_broadcast([16, 8, 128]), ALU.add, [mixs, wsc], [mixs])
                for g in range(8):
                    fw.op("pe", lambda g=g: T.transpose(WK[2][:, g * 16:(g + 1) * 16], mixs[0:16, g * 128:(g + 1) * 128], ident[0:16, 0:16]),
                          reads=[mixs, ident], writes=[WK[2]], waw=(g == 0), inc=(g == 7))
                cp("dve", mixsT[:], WK[2][:, 0:128].rearrange("p (g s) -> p g s", s=16), [WK[2]], [mixsT])
        for g in range(8):
            wu = load_w(w_in, 128 * g); wz = load_w(w_in, 2048 + 128 * g)
            for ci in range(4):
                mm(WK[0][:, ci * 128:(ci + 1) * 128], vn[:, ci, g * 128:(g + 1) * 128], wm[:, g, :], True, True, [vn, wm], [WK[0]])
            tt("dve", gtmp2[:, 0:512].rearrange("p (c t) -> p c t", t=128), WK[0][:, :].rearrange("p (c t) -> p c t", t=128),
               bsb[:, g * 128:(g + 1) * 128].unsqueeze(1).to_broadcast([128, 4, 128]), ALU.add, [WK[0], bsb], [gtmp2])
            inproj(PJ, wu, 0, hT, 1, 1 + nf)
            for bi, (a, b) in enumerate(blocks(0, nf)):
                gelu_from(PJ[bi][:, 0:b - a], 128, b - a, [PJ[bi]], tB, tB[:, a:b])
            tt("dve", tB[:, 0:TO], tB[:, 0:TO], gtmp2[:, 0:TO], ALU.mult, [tB, gtmp2], [tB])
            if ns:
                tt("dve", tB[:, TO:nf], tB[:, TO:nf], mixsT[:, g, :], ALU.mult, [tB, mixsT], [tB])
            inproj(PJ, wz, 0, hT, 1, 1 + nf)
            for bi, (a, b) in enumerate(blocks(0, nf)):
                act(tC[:, a:b], PJ[bi][:, 0:b - a], AF.Silu, [PJ[bi]], [tC], waw=(bi == 0))
            tt("dve", ya_T[:, g, 0:nf], tB[:, 0:nf], tC[:, 0:nf], ALU.mult, [tB, tC], [ya_T], waw=False)
        fw.release(mG)
        for dt_ in range(16):
            wga = load_w(w_in, C_GA + 128 * dt_)
            inproj(PJ, wga, 0, hT, 1, 1 + nf)
            for bi, (a, b) in enumerate(blocks(0, nf)):
                act(tA[:, a:b], PJ[bi][:, 0:b - a], AF.Sigmoid, [PJ[bi]], [tA], waw=(bi == 0))
            wpa = load_w(p_a, 128 * dt_, 128, rows=1024)
            inproj(PJ, wpa, 0, ya_T, 0, nf, nk=8)
            for bi, (a, b) in enumerate(blocks(0, nf)):
                tt("dve", tA[:, a:b], tA[:, a:b], PJ[bi][:, 0:b - a], ALU.mult, [tA, PJ[bi]], [tA], waw=False)
            wgb = load_w(w_in, C_GB + 128 * dt_)
            inproj(PJ, wgb, 0, hT, 1, 1 + nf)
            for bi, (a, b) in enumerate(blocks(0, nf)):
                act(tB[:, a:b], PJ[bi][:, 0:b - a], AF.Sigmoid, [PJ[bi]], [tB], waw=(bi == 0))
            wpb = load_w(p_b, 128 * dt_)
            inproj(PJ, wpb, 0, yb_T, 0, nf)
            for bi, (a, b) in enumerate(blocks(0, nf)):
                tt("dve", tB[:, a:b], tB[:, a:b], PJ[bi][:, 0:b - a], ALU.mult, [tB, PJ[bi]], [tB], waw=False)
            tt("dve", mg_T[:, dt_, 0:nf], tA[:, 0:nf], tB[:, 0:nf], ALU.add, [tA, tB], [mg_T], waw=False)
        mO = fw.mark()
        fgb = fw.sb("fgb", [128, D])
        fw.dma("sp", fgb[:], final_g[0:1, :].to_broadcast([128, D]), reads=[final_g], writes=[fgb])
        ob5 = [fw.sb(f"ob5_{i}", [128, D]) for i in range(nch)]
        alloc_x()
        xt = XH["xt"]; junk = XH["junk"]; rs_ring = XH["rs"]
        for cb in range(4):
            for q in range(4):
                wt = load_w(w_out, 512 * cb + 128 * q)
                for ci in range(nch):
                    n = 128 if ci < 4 else 16
                    c0 = 128 * ci
                    for k in range(16):
                        mm(bank5[ci][0:n, q * 128:(q + 1) * 128], mg_T[:, k, c0:c0 + n], wt[:, k, :], k == 0, k == 15, [mg_T, wt], [bank5[ci]])
            for ci in range(nch):
                n = 128 if ci < 4 else 16
                P = slice(0, n)
                gsrc, gb_ = (gate_bc[P, cb * 512:(cb + 1) * 512], gate_bc) if ci < 4 else (gate_s[0:16, cb * 512:(cb + 1) * 512], gate_s)
                tt("dve", ob5[ci][P, cb * 512:(cb + 1) * 512], bank5[ci][P, :], gsrc, ALU.mult, [bank5[ci], gb_], [ob5[ci]], waw=False)
        for ci in range(nch):
            n = 128 if ci < 4 else 16
            P = slice(0, n)
            c0 = 128 * ci
            ob = ob5[ci]
            if ci < 4:
                fw.dma("sp", xt[P, :], x_own[seg * TO + c0:seg * TO + c0 + n, :], reads=[x_own], writes=[xt])
            else:
                fw.dma("sp", xt[P, :], x_misc[3, 0:16, :], reads=[x_misc], writes=[xt])
            tt("dve", ob[P, :], ob[P, :], xt[P, :], ALU.add, [ob, xt], [ob])
            rs = rs_ring[ci % 2]
            fw.op("act", lambda: A.activation(junk[P, :], ob[P, :], AF.Square, accum_out=rs[P, 0:1]), reads=[ob], writes=[junk, rs])
            ts("dve", rs[P, 1:2], rs[P, 0:1], 1.0 / D, NORM_EPS, ALU.mult, ALU.add, [rs], [rs])
            act(rs[P, 2:3], rs[P, 1:2], AF.Sqrt, [rs], [rs])
            fw.op("dve", lambda: V.reciprocal(rs[P, 3:4], rs[P, 2:3]), reads=[rs], writes=[rs])
            stt(ob[P, :], ob[P, :], rs[P, 3:4], fgb[P, :], ALU.mult, ALU.mult, [ob, rs, fgb], [ob])
            if ci < 4:
                fw.dma("sp", y_own[seg * TO + c0:seg * TO + c0 + n, :], ob[P, :], reads=[ob], writes=[y_own], waw=False)
            else:
                fw.dma("sp", y_s[:], ob[P, :], reads=[ob], writes=[y_s], waw=False)
        fw.release(mO)

    for ph in ("P", "R"):
        for seg in range(2):
            if STOP == "P" and ph == "R":
                return done()
            if STOP == "R0" and ph == "R" and seg == 1:
                return done()
            ns = TS if (ph == "R" and seg == 1) else 0
            xsrc = x_pre if ph == "P" else x_own
            mh = fw.mark()
            alloc_x()
            for i in range(4):
                try:
                    build_hT(hT, 1 + 128 * i, xsrc, seg * TO + 128 * i, 128)
                except StopBuild:
                    return done()
                if STOP == "h1":
                    return done()
            if STOP == "h4":
                return done()
            build_hT(hT, 1 + TO, x_misc, 0, 17, is_misc=True, ns=ns, xsel=(0 if ph == "P" else 2) + seg)
            if STOP == "h":
                return done()
            fw.release(mh)
            mfac = {("P", 0): zero1, ("P", 1): one1, ("R", 0): mk, ("R", 1): one1}[(ph, seg)]
            hfac = mk if (ph == "R" and seg == 0) else one1
            m1 = fw.mark()
            rp, fin = make_rwkv()
            rp(hT, TO, ns, ph == "R", mfac, hfac, ph == "R" and seg == 1)
            if ph == "R" and seg == 1:
                fin()
            fw.release(m1)
            if STOP == "P0":
                return done()
            if ph == "R" and STOP != "nogmo":
                m2 = fw.mark()
                gmo_segment(seg, ns)
                fw.release(m2)
    fw.finish()
    fw.close()
    return nc


_CACHE = {}


def kernel(x_prompt, x_sample, c_prompt, c_sample, state_wkv, state_shift,
           norm_g, w_c, b_c, w_in, ln_v_g, ln_v_b, w_s, b_s,
           mu_shift, w0, w2, a0, a2, k_k, k_a, r_k, gn_g, gn_b,
           p_a, p_b, w_out, final_g):
    f = lambda a: np.ascontiguousarray(np.asarray(a, dtype=np.float32))
    x_prompt, x_sample, c_prompt, c_sample = f(x_prompt), f(x_sample), f(c_prompt), f(c_sample)
    state_wkv, state_shift = f(state_wkv), f(state_shift)
    if "nc" not in _CACHE:
        _CACHE["nc"] = build_program()
    nc = _CACHE["nc"]
    mu = f(mu_shift)[0]
    t16 = lambda v: f(v).reshape(16, 128).T
    pc = np.zeros((128, NPC), np.float32)
    cols = [f(norm_g)[0], mu[0:2048], mu[2144:4192], mu[4192:6240], f(w0)[0], f(a0)[0], f(k_k)[0], f(k_a)[0],
            f(r_k)[0].reshape(-1), f(gn_g)[0], f(gn_b)[0]]
    for i, v in enumerate(cols):
        pc[:, 16 * i:16 * i + 16] = t16(v)
    pc[0:96, PC_MUWD] = mu[2048:2144]
    pc[0:96, PC_MUAD] = mu[6240:6336]
    shared = {
        "pc": pc, "w_c": f(w_c)[0], "b_c": f(b_c)[0][None, :], "w_in": f(w_in)[0],
        "ln_v_g": f(ln_v_g)[0][None, :], "ln_v_b": f(ln_v_b)[0][None, :],
        "w_sT": np.ascontiguousarray(f(w_s)[0].transpose(0, 2, 1)), "b_s": f(b_s)[0].reshape(1, 1024),
        "ws00": np.ascontiguousarray(f(w_s)[0][:, 0, 0])[None, :], "bs0": np.ascontiguousarray(f(b_s)[0][:, 0])[None, :],
        "w2": f(w2)[0], "a2": f(a2)[0], "p_a": f(p_a)[0], "p_b": f(p_b)[0], "w_out": f(w_out)[0],
        "final_g": f(final_g)[None, :],
    }
    in_maps = []
    for c in range(8):
        b, hh = c // 2, c % 2
        xs = x_sample[16 * c:16 * c + 16, 0]
        x_pre = x_prompt[b, 0:1024]
        x_own = x_prompt[b, 1024 * hh:1024 * hh + 1024]
        xm = np.zeros((4, 17, 2048), np.float32)
        xm[1, 16] = x_pre[511]
        xm[2, 16] = x_pre[1023]
        xm[3, 0:16] = xs
        xm[3, 16] = x_own[511]
        cin = np.concatenate([c_sample[16 * c:16 * c + 16], c_prompt[b:b + 1]], 0)
        sw = state_wkv[0, 16 * c:16 * c + 16].reshape(16, 16, 2, 64, 64)
        sw = np.ascontiguousarray(sw.transpose(1, 2, 4, 0, 3)).reshape(16, 128, 16, 64)
        m = dict(shared)
        m.update({"x_pre": np.ascontiguousarray(x_pre), "x_own": np.ascontiguousarray(x_own), "x_misc": xm,
                  "cin": np.ascontiguousarray(cin), "maskv": np.full((128, 1), float(hh), np.float32),
                  "swkv": sw, "sshift": np.ascontiguousarray(state_shift[0, 16 * c:16 * c + 16])})
        in_maps.append(m)
    res = run_bass_kernel_spmd(nc, in_maps, core_ids=list(range(8))).results
    y_prompt = np.zeros((4, 2048, 2048), np.float32)
    y_sample = np.zeros((128, 1, 2048), np.float32)
    wkv_prompt = np.zeros((1, 4, 32, 64, 64), np.float32)
    shift_prompt = np.zeros((1, 4, 6336), np.float32)
    wkv_sample = np.zeros((1, 128, 32, 64, 64), np.float32)
    shift_sample = np.zeros((1, 128, 6336), np.float32)
    cv = np.zeros((1, 128, 1, 1024), np.float32)

    def unshift(a):
        tail = a.shape[2:]
        r = a[:, 0:16].transpose(1, 0, *range(2, a.ndim)).reshape(2048, *tail)
        k = a[:, 16:32].transpose(1, 0, *range(2, a.ndim)).reshape(2048, *tail)
        v = a[:, 32:48].transpose(1, 0, *range(2, a.ndim)).reshape(2048, *tail)
        wd = a[0:96, 48]
        ad = a[0:96, 49]
        return np.concatenate([r, wd, k, v, ad], 0)

    for c in range(8):
        b, hh = c // 2, c % 2
        r = res[c]
        y_prompt[b, 1024 * hh:1024 * hh + 1024] = r["y_own"]
        y_sample[16 * c:16 * c + 16, 0] = r["y_s"]
        ws = r["wkv_s"].reshape(16, 2, 64, 16, 64)
        wkv_sample[0, 16 * c:16 * c + 16] = ws.transpose(3, 0, 1, 4, 2).reshape(16, 32, 64, 64)
        shift_sample[0, 16 * c:16 * c + 16] = unshift(r["sh_s"]).T
        cv[0, 16 * c:16 * c + 16, 0] = r["cv_s"]
        if hh == 1:
            wp = r["wkv_p"].reshape(16, 2, 64, 64)
            wkv_prompt[0, b] = wp.transpose(0, 1, 3, 2).reshape(32, 64, 64)
            shift_prompt[0, b] = unshift(r["sh_p"])
    return (y_prompt, y_sample, wkv_prompt, shift_prompt, wkv_sample, shift_sample, cv)
```

```python
import numpy as np
import concourse.bass as bass
import concourse.mybir as mybir

F32 = mybir.dt.float32
BF16 = mybir.dt.bfloat16
AF = mybir.ActivationFunctionType
ALU = mybir.AluOpType
AX = mybir.AxisListType


class Buf:
    __slots__ = ("t", "w", "r", "name", "excl")

    def __init__(self, t, name=""):
        self.t = t
        self.excl = False
        self.w = {}
        self.r = {}
        self.name = name

    def __getitem__(self, idx):
        return self.t[idx]


class Eng:
    def __init__(self, fw, name, h, sem):
        self.fw = fw
        self.name = name
        self.h = h
        self.sem = sem
        self.cnt = 0
        self.waited = {}
        self.pend_r = []
        self.pend_w = []


class FW:
    def __init__(self, nc, n_dma_sems=12):
        self.nc = nc
        self._ctx = []
        self.sems = {}
        self.E = {}
        for name, h in (("pe", nc.tensor), ("act", nc.scalar), ("dve", nc.vector),
                        ("pool", nc.gpsimd), ("sp", nc.sync)):
            s = self.enter(nc.semaphore("sem_" + name))
            self.sems[id(s)] = s
            self.E[name] = Eng(self, name, h, s)
        self.dma_ring = {}
        for q in ("sp", "pool", "act"):
            ring = []
            for i in range(n_dma_sems):
                s = self.enter(nc.semaphore(f"dsem_{q}{i}"))
                self.sems[id(s)] = s
                ring.append([s, 0])
            self.dma_ring[q] = [ring, 0]
        self.n_inst = 0

    def enter(self, cm):
        v = cm.__enter__()
        self._ctx.append(cm)
        return v

    def mark(self):
        return len(self._ctx)

    def release(self, m):
        self.barrier()
        while len(self._ctx) > m:
            self._ctx.pop().__exit__(None, None, None)

    def close(self):
        for cm in reversed(self._ctx):
            cm.__exit__(None, None, None)
        self._ctx = []

    def sb(self, name, shape, dt=F32):
        self._uid = getattr(self, "_uid", 0) + 1
        return Buf(self.enter(self.nc.sbuf_tensor(f"{name}_{self._uid}", list(shape), dt)), name)

    def barrier(self):
        for e in self.E.values():
            for o in self.E.values():
                if o.cnt > 0:
                    self._wait(e, o.sem, o.cnt)
            for q, (ring, pos) in self.dma_ring.items():
                for sem, val in ring:
                    if val > 0:
                        self._wait(e, sem, val)

    def ps(self, name, shape, dt=F32):
        b = Buf(self.enter(self.nc.psum_tensor(name, list(shape), dt)), name)
        b.excl = True
        return b

    def _wait(self, e, sem, val):
        k = id(sem)
        if e.waited.get(k, 0) >= val:
            return
        e.h.wait_ge(sem, val)
        e.waited[k] = val
        self.n_inst += 1

    def _deps(self, e, reads, writes, waw=True):
        need = {}
        for b in reads:
            for k, v in b.w.items():
                if need.get(k, 0) < v:
                    need[k] = v
            if b.excl:
                own = id(e.sem)
                for k, v in b.r.items():
                    if k != own and need.get(k, 0) < v:
                        need[k] = v
        for b in writes:
            for k, v in b.r.items():
                if need.get(k, 0) < v:
                    need[k] = v
            if waw:
                for k, v in b.w.items():
                    if need.get(k, 0) < v:
                        need[k] = v
        for k, v in need.items():
            self._wait(e, self.sems[k], v)

    def op(self, eng, fn, reads=(), writes=(), inc=True, waw=True):
        e = self.E[eng]
        self._deps(e, reads, writes, waw)
        ins = fn()
        self.n_inst += 1
        if inc:
            e.cnt += 1
            ins.then_inc(e.sem, 1)
            k = id(e.sem)
            rs = list(reads) + e.pend_r
            ws = list(writes) + e.pend_w
            e.pend_r = []
            e.pend_w = []
            for b in rs:
                b.r[k] = e.cnt
            for b in ws:
                if waw and b in writes:
                    pass
                b.w[k] = e.cnt
        else:
            e.pend_r += list(reads)
            e.pend_w += list(writes)
        return ins

    def fresh(self, b):
        pass

    def dma(self, q, out_ap, in_ap, reads=(), writes=(), waw=True, **kw):
        e = self.E[q]
        ring, pos = self.dma_ring[q]
        slot = ring[pos % len(ring)]
        self.dma_ring[q][1] = pos + 1
        sem, val = slot
        if val > 0:
            self._wait(e, sem, val)
        self._deps(e, reads, writes, waw)
        ins = e.h.dma_start(out=out_ap, in_=in_ap, **kw)
        self.n_inst += 1
        slot[1] = val + 16
        ins.then_inc(sem, 16)
        k = id(sem)
        for b in reads:
            b.r[k] = slot[1]
        for b in writes:
            b.w[k] = slot[1]
        return ins

    def finish(self):
        e = self.E["sp"]
        for q, (ring, pos) in self.dma_ring.items():
            for sem, val in ring:
                if val > 0:
                    self._wait(e, sem, val)
        for name, en in self.E.items():
            if name != "sp" and en.cnt > 0:
                self._wait(e, en.sem, en.cnt)
from concourse.bass_utils import run_bass_kernel_spmd
TO = 512
TP = 512
TS = 16
D = 2048
NCH = 64
NF = TO + TS
C_R, C_WD, C_K, C_V, C_AD = 3072, 5120, 5216, 7264, 9312
C_ZB, C_GA, C_GB = 9408, 11456, 13504
EXPM05 = 0.6065306597126334
GN_EPS = 64e-5
NORM_EPS = 1e-6
PC_NORMG, PC_MUR, PC_MUK, PC_MUV, PC_W0, PC_A0, PC_KK, PC_KA, PC_RK, PC_GNG, PC_GNB = [16 * i for i in range(11)]
PC_MUWD, PC_MUAD = 176, 177
NPC = 178


def blocks(c0, c1, w=512):
    out = []
    a = c0
    while a < c1:
        b = min(a + w, c1)
        out.append((a, b))
        a = b
    return out


class StopBuild(Exception):
    pass


def build_program():
    nc = bass.Bass("TRN2", target_bir_lowering=False)
    fw = FW(nc)
    V, A, G, T = nc.vector, nc.scalar, nc.gpsimd, nc.tensor

    def din(name, shape, dt=F32):
        return Buf(nc.dram_tensor(name, list(shape), dt, kind="ExternalInput").ap(), name)

    def dout(name, shape, dt=F32):
        return Buf(nc.dram_tensor(name, list(shape), dt, kind="ExternalOutput").ap(), name)

    x_pre = din("x_pre", [1024, D]); x_own = din("x_own", [1024, D]); x_misc = din("x_misc", [4, 17, D])
    cin = din("cin", [17, D]); maskv = din("maskv", [128, 1]); pc_d = din("pc", [128, NPC])
    swkv = din("swkv", [16, 128, TS, 64]); sshift = din("sshift", [TS, 6336])
    w_c = din("w_c", [D, 6144]); b_c = din("b_c", [1, 6144]); w_in = din("w_in", [D, 15552])
    lnv_g = din("ln_v_g", [1, 1024]); lnv_b = din("ln_v_b", [1, 1024])
    w_sT = din("w_sT", [8, 128, 128]); b_s = din("b_s", [1, 1024]); ws00 = din("ws00", [1, 8]); bs0 = din("bs0", [1, 8])
    w2 = din("w2", [96, D]); a2 = din("a2", [96, D])
    p_a = din("p_a", [1024, D]); p_b = din("p_b", [D, D]); w_out = din("w_out", [D, D]); final_g = din("final_g", [1, D])

    y_own = dout("y_own", [1024, D]); y_s = dout("y_s", [TS, D])
    wkv_p = dout("wkv_p", [16, 128, 64]); sh_p = dout("sh_p", [128, 50])
    wkv_s = dout("wkv_s", [16, 128, TS, 64]); sh_s = dout("sh_s", [128, 50, TS]); cv_s = dout("cv_s", [TS, 1024])

    def tt(eng, out, in0, in1, op, R, W, **kw):
        h = fw.E[eng].h
        return fw.op(eng, lambda: h.tensor_tensor(out, in0, in1, op), reads=R, writes=W, **kw)

    def ts(eng, out, in0, s1, s2, op0, op1, R, W, **kw):
        h = fw.E[eng].h
        if op1 is None:
            return fw.op(eng, lambda: h.tensor_scalar(out, in0, s1, None, op0), reads=R, writes=W, **kw)
        return fw.op(eng, lambda: h.tensor_scalar(out, in0, s1, s2, op0, op1), reads=R, writes=W, **kw)

    def stt(out, in0, sc, in1, op0, op1, R, W, **kw):
        return fw.op("dve", lambda: V.scalar_tensor_tensor(out, in0, sc, in1, op0, op1), reads=R, writes=W, **kw)

    def act(out, in_, func, R, W, bias=None, scale=None, **kw):
        def f():
            kws = {}
            if bias is not None:
                kws["bias"] = bias
            if scale is not None:
                kws["scale"] = scale
            return A.activation(out, in_, func, **kws)
        return fw.op("act", f, reads=R, writes=W, **kw)

    def cp(eng, out, in_, R, W, **kw):
        if eng == "act":
            return fw.op("act", lambda: A.copy(out, in_), reads=R, writes=W, **kw)
        h = fw.E[eng].h
        return fw.op(eng, lambda: h.tensor_copy(out, in_), reads=R, writes=W, **kw)

    def mm(out, lhsT, rhs, start, stop, R, W, inc=None):
        if inc is None:
            inc = stop
        return fw.op("pe", lambda: T.matmul(out, lhsT, rhs, start=start, stop=stop), reads=R, writes=W, inc=inc, waw=start)

    ident = fw.sb("ident", [128, 128]); ones_t = fw.sb("ones_t", [128, 128])
    bones = fw.sb("bones", [128, 128]); bones64 = fw.sb("bones64", [128, 128])
    ident2 = fw.sb("ident2", [128, 64])
    mask5 = fw.sb("mask5", [128, 5, 64]); triu = fw.sb("triu", [128, 128])
    zero1 = fw.sb("zero1", [128, 1])
    fw.op("pool", lambda: G.memset(ones_t[:], 1.0), writes=[ones_t])
    fw.op("pool", lambda: G.memset(zero1[:], 0.0), writes=[zero1])
    fw.op("pool", lambda: G.affine_select(out=ident[:], in_=ones_t[:], pattern=[[1, 128]], compare_op=ALU.is_equal,
                                          fill=0.0, base=0, channel_multiplier=-1), reads=[ones_t], writes=[ident])
    fw.op("pool", lambda: G.affine_select(out=triu[:], in_=ones_t[:], pattern=[[1, 128]], compare_op=ALU.is_ge,
                                          fill=0.0, base=0, channel_multiplier=-1), reads=[ones_t], writes=[triu])
    fw.op("pool", lambda: G.memset(bones[:], 0.0), writes=[bones])
    fw.op("pool", lambda: G.memset(bones[0:64, 0:64], 1.0), writes=[bones], waw=True)
    fw.op("pool", lambda: G.memset(bones[64:128, 64:128], 1.0), writes=[bones], waw=True)
    ts("pool", bones64[:], bones[:], 1.0 / 64.0, None, ALU.mult, None, [bones], [bones64])
    bones_bf = fw.sb("bones_bf", [128, 128], BF16)
    cp("pool", bones_bf[:], bones[:], [bones], [bones_bf])
    tt("pool", ident2[:], ident[:, 0:64], ident[:, 64:128], ALU.add, [ident], [ident2])
    for hp_ in range(2):
        ps_ = slice(64 * hp_, 64 * hp_ + 64)
        for x_ in range(5):
            cmp_, pat, cm = {0: (ALU.is_gt, 1, -1), 1: (ALU.is_ge, 1, -1), 2: (ALU.is_gt, 1, -1),
                             3: (ALU.is_ge, 1, -1), 4: (ALU.is_gt, -1, 1)}[x_]
            fw.op("pool", lambda ps_=ps_, x_=x_, cmp_=cmp_, pat=pat, cm=cm: G.affine_select(
                out=mask5[ps_, x_, :], in_=ones_t[ps_, 0:64], pattern=[[pat, 64]], compare_op=cmp_,
                fill=0.0, base=0, channel_multiplier=cm), reads=[ones_t], writes=[mask5], waw=False)

    import os
    STOP = os.environ.get("KSTOP", "")
    def done():
        fw.finish(); fw.close(); return nc
    if STOP == "k1":
        return done()
    pc = fw.sb("pc_sb", [128, NPC]); mk = fw.sb("mk", [128, 1])
    fw.dma("sp", pc[:], pc_d[:], reads=[pc_d], writes=[pc])
    fw.dma("sp", mk[:], maskv[:], reads=[maskv], writes=[mk])
    gneps = fw.sb("gneps", [128, 1]); neps = fw.sb("neps", [128, 1]); one1 = fw.sb("one1", [128, 1])
    fw.op("pool", lambda: G.memset(gneps[:], GN_EPS), writes=[gneps])
    fw.op("pool", lambda: G.memset(neps[:], NORM_EPS), writes=[neps])
    fw.op("pool", lambda: G.memset(one1[:], 1.0), writes=[one1])

    if STOP == "k2":
        return done()
    PJ = [fw.ps(f"pj{i}", [128, 512]) for i in range(3)]
    WK = [fw.ps(f"wk{i}", [128, 512]) for i in range(5)]

    wring = [fw.sb(f"wslot{i}", [128, 16, 128], BF16) for i in range(5)]
    wpos = [0]

    def load_w(src, c0, width=128, rows=D):
        slot = wring[wpos[0] % len(wring)]; wpos[0] += 1
        nk = rows // 128
        fw.dma("pool", slot[:, 0:nk, 0:width], src[0:rows, c0:c0 + width].rearrange("(k p) n -> p k n", p=128),
               reads=[src], writes=[slot])
        return slot

    def inproj(pss, wt, coff, hT, c0, c1, nk=16):
        for bi, (a, b) in enumerate(blocks(c0, c1)):
            for k in range(nk):
                mm(pss[bi][:, 0:b - a], wt[:, k, coff:coff + 128], hT[:, k, a:b], k == 0, k == nk - 1,
                   [wt, hT], [pss[bi]])
    sel16 = fw.sb("sel16", [17, 128])
    fw.op("pool", lambda: G.affine_select(out=sel16[:], in_=ones_t[0:17, :], pattern=[[0, 128]], compare_op=ALU.is_ge,
                                          fill=0.0, base=-16, channel_multiplier=1), reads=[ones_t], writes=[sel16])
    modT = fw.sb("modT", [128, 32, 17]); gsc = fw.sb("gsc", [128, 16, 17]); gate_bc = fw.sb("gate_bc", [128, D]); gate_s = fw.sb("gate_s", [16, D])
    m0 = fw.mark()
    cin_sb = fw.sb("cin_sb", [17, D])
    fw.dma("sp", cin_sb[:], cin[:], reads=[cin], writes=[cin_sb])
    cT = fw.sb("cT", [128, 16, 17], BF16)
    for k in range(16):
        fw.op("pe", lambda k=k: T.transpose(WK[0][:, k * 17:(k + 1) * 17], cin_sb[0:17, k * 128:(k + 1) * 128], ident[0:17, 0:17]),
              reads=[cin_sb, ident], writes=[WK[0]], waw=(k == 0), inc=(k == 15))
    cp("dve", cT[:], WK[0][:, 0:16 * 17].rearrange("p (k t) -> p k t", t=17), [WK[0]], [cT])
    if STOP == "k3":
        return done()
    mod = fw.sb("mod", [17, 6144])
    bcr = [fw.sb(f"bcr{i}", [17, 512]) for i in range(2)]
    wbig0 = [fw.sb(f"wbig0_{i}", [128, 16, 512], BF16) for i in range(2)]
    for cb in range(12):
        bcb = bcr[cb % 2]
        fw.dma("sp", bcb[:], b_c[0:1, cb * 512:(cb + 1) * 512].to_broadcast([17, 512]), reads=[b_c], writes=[bcb])
        ps = PJ[cb % 3]
        wt = wbig0[cb % 2]
        fw.dma("pool", wt[:, :, :], w_c[:, cb * 512:(cb + 1) * 512].rearrange("(k p) n -> p k n", p=128), reads=[w_c], writes=[wt])
        for k in range(16):
            mm(ps[0:17, :], cT[:, k, :], wt[:, k, :], k == 0, k == 15, [cT, wt], [ps])
        tt("dve", mod[:, cb * 512:(cb + 1) * 512], ps[0:17, :], bcb[:], ALU.add, [ps, bcb], [mod], waw=False)
    if STOP == "k4":
        return done()
    for half in range(2):
        ps = WK[1 + half]
        for t in range(16):
            tg = half * 16 + t
            fw.op("pe", lambda t=t, tg=tg, ps=ps: T.transpose(ps[:, t * 17:(t + 1) * 17], mod[0:17, tg * 128:(tg + 1) * 128], ident[0:17, 0:17]),
                  reads=[mod, ident], writes=[ps], waw=(t == 0), inc=(t == 15))
        cp("dve", modT[:, half * 16:(half + 1) * 16, :], ps[:, 0:16 * 17].rearrange("p (k t) -> p k t", t=17), [ps], [modT], waw=False)
    if STOP == "k5":
        return done()
    ts("dve", gsc[:], modT[:, 16:32, :], 1.0, None, ALU.add, None, [modT], [gsc])
    tt("dve", gsc[:], gsc[:], pc[:, PC_NORMG:PC_NORMG + 16].unsqueeze(2).to_broadcast([128, 16, 17]), ALU.mult, [gsc, pc], [gsc])
    if STOP == "k6":
        return done()
    for cb in range(4):
        ps = PJ[cb % 3]
        mm(ps[:, :], sel16[:, :], mod[0:17, 4096 + cb * 512:4096 + (cb + 1) * 512], True, True, [sel16, mod], [ps])
        cp("act", gate_bc[:, cb * 512:(cb + 1) * 512], ps[:, :], [ps], [gate_bc], waw=False)

    if STOP == "k7":
        return done()
    cp("dve", gate_s[:], mod[0:16, 4096:6144], [mod], [gate_s])
    if STOP == "k8":
        return done()
    if STOP == "k9":
        fw.barrier()
        return done()
    if STOP == "k10":
        while len(fw._ctx) > m0:
            fw._ctx.pop().__exit__(None, None, None)
        return done()
    fw.release(m0)
    xpos = [0]
    XH = {}

    def alloc_x(nx=1):
        XH["xts"] = [fw.sb(f"xt{i}", [128, D]) for i in range(nx)]
        XH["xt"] = XH["xts"][0]; XH["rs"] = [fw.sb(f"rs{i}", [128, 4]) for i in range(2)]
        XH["junk"] = fw.sb("junk", [128, D], BF16); XH["xt_tmp"] = fw.sb("xt_tmp", [128, 16])

    def build_hT(hT, col0, xsrc, r0, nrows, is_misc=False, ns=16, xsel=None):
        xt = XH["xts"][xpos[0] % len(XH["xts"])]; rs = XH["rs"][xpos[0] % 2]; xpos[0] += 1; junk = XH["junk"]; xt_tmp = XH["xt_tmp"]
        n = nrows
        fw.dma("sp", xt[0:n, :], (xsrc[xsel] if xsel is not None else xsrc[r0:r0 + n, :]), reads=[xsrc], writes=[xt])
        if STOP == "hs1":
            raise StopBuild()
        fw.op("act", lambda: A.activation(junk[0:n, :], xt[0:n, :], AF.Square, accum_out=rs[0:n, 0:1]), reads=[xt], writes=[junk, rs])
        if STOP == "hs2":
            raise StopBuild()
        ts("dve", rs[0:n, 1:2], rs[0:n, 0:1], 1.0 / D, NORM_EPS, ALU.mult, ALU.add, [rs], [rs])
        act(rs[0:n, 2:3], rs[0:n, 1:2], AF.Sqrt, [rs], [rs])
        fw.op("dve", lambda: V.reciprocal(rs[0:n, 3:4], rs[0:n, 2:3]), reads=[rs], writes=[rs])
        if STOP == "hs3":
            raise StopBuild()
        ts("dve", xt[0:n, :], xt[0:n, :], rs[0:n, 3:4], None, ALU.mult, None, [xt, rs], [xt])
        if STOP == "hs4":
            raise StopBuild()
        for q in range(4):
            ps = PJ[q] if q < 3 else WK[0]
            for kk_ in range(4):
                k = q * 4 + kk_
                fw.op("pe", lambda k=k, kk_=kk_, ps=ps: T.transpose(ps[:, kk_ * 128:kk_ * 128 + n], xt[0:n, k * 128:(k + 1) * 128], ident[0:n, 0:n]),
                      reads=[xt, ident], writes=[ps], waw=(kk_ == 0), inc=(kk_ == 3))
            if STOP == "hs5":
                raise StopBuild()
            for kk_ in range(4):
                k = q * 4 + kk_
                eng = "dve" if q % 2 == 0 else "pool"
                if not is_misc:
                    if eng == "pool":
                        act(hT[:, k, col0:col0 + n], ps[:, kk_ * 128:kk_ * 128 + n], AF.Identity, [ps, gsc, modT], [hT],
                            bias=modT[:, k, 16:17], scale=gsc[:, k, 16:17], waw=False)
                    else:
                        ts("dve", hT[:, k, col0:col0 + n], ps[:, kk_ * 128:kk_ * 128 + n], gsc[:, k, 16:17], modT[:, k, 16:17],
                           ALU.mult, ALU.add, [ps, gsc, modT], [hT], waw=False)
                else:
                    if ns:
                        tt("dve", xt_tmp[:, 0:16], ps[:, kk_ * 128:kk_ * 128 + 16], gsc[:, k, 0:16], ALU.mult, [ps, gsc], [xt_tmp])
                        tt("dve", hT[:, k, col0:col0 + 16], xt_tmp[:, 0:16], modT[:, k, 0:16], ALU.add, [xt_tmp, modT], [hT], waw=False)
                    ts("dve", hT[:, k, 0:1], ps[:, kk_ * 128 + 16:kk_ * 128 + 17], gsc[:, k, 16:17], modT[:, k, 16:17],
                       ALU.mult, ALU.add, [ps, gsc, modT], [hT], waw=False)

    def make_rwkv():
        NF = TO + TS
        raw = fw.sb("raw", [128, NF + 1]); tmpA = fw.sb("tmpA", [128, NF]); tmpB = fw.sb("tmpB", [128, NF]); tmpC = fw.sb("tmpC", [128, NF])
        SETS = []
        for si in range(2):
            SETS.append((fw.sb("AR", [128, 2, NF]), fw.sb("BKt", [128, 2, NF]), fw.sb("vT", [128, NF]), fw.sb("eL", [128, NF]),
                         fw.sb("lw", [128, NF]), fw.sb("bonus", [128, NF]), fw.sb("zbS", [128, NF], BF16), fw.sb("ys16", [128, 16])))
        Lc = fw.sb("Lc", [128, NF]); YT = fw.sb("YT", [128, NF])
        ws16 = fw.sb("ws16", [128, 16]); st1b = fw.sb("st1b", [128, TS, 64], BF16)
        tA2 = fw.sb("tA2", [128, NF]); tB2 = fw.sb("tB2", [128, NF]); tC2 = fw.sb("tC2", [128, NF])
        l2ring = [fw.sb(f"l2r{i}", [96, 128], BF16) for i in range(6)]
        l2pos = [0]

        def load_small(src, hp):
            t_ = l2ring[l2pos[0] % len(l2ring)]; l2pos[0] += 1
            fw.dma("pool", t_[:, :], src[0:96, hp * 128:(hp + 1) * 128], reads=[src], writes=[t_])
            return t_
        rmask = fw.sb("rmask", [128, TO])
        fw.op("dve", lambda: V.memset(rmask[:], 1.0), writes=[rmask])
        fw.op("dve", lambda: V.memset(rmask[:].rearrange("p (c t) -> p c t", t=NCH)[:, :, 0:1], 0.0), writes=[rmask])
        thw = fw.sb("thw", [96, NF], BF16); adx = fw.sb("adx", [96, NF], BF16)
        Hst = fw.sb("Hst", [128, 64])
        tk8 = fw.sb("tk8", [128, 8, 5, 64]); sc8 = fw.sb("sc8", [128, 8, 5, 64]); qp8 = fw.sb("qp8", [128, 8, 3, 64])
        AW8 = fw.sb("AW8", [128, 8, 2, 64]); U08 = fw.sb("U08", [128, 8]); GT8 = fw.sb("GT8", [128, 8, 64])
        ZM8 = fw.sb("ZM8", [128, 8, 2, 64]); Qy8 = fw.sb("Qy8", [128, 8, 64]); H08 = fw.sb("H08", [128, 9, 64])
        shp_sb = fw.sb("shp_sb", [128, 50]); shs_sb = fw.sb("shs_sb", [128, 50, TS])
        fw.op("dve", lambda: V.memset(shp_sb[:], 0.0), writes=[shp_sb])
        fw.op("dve", lambda: V.memset(shs_sb[:], 0.0), writes=[shs_sb])
        sst = fw.sb("sst", [TS, 128])
        Hs = fw.sb("Hs", [128, TS, 64]); st1 = fw.sb("st1", [128, TS, 64]); st2 = fw.sb("st2", [128, TS, 64])

        def shifted(ps3, n, ns, mu_ap, obuf, out_ap_fn, tile_idx, col_c0, with_out, mfac, nparts=128):
            P = slice(0, nparts)
            tot = 1 + n + ns
            for bi, (a, b) in enumerate(blocks(0, tot)):
                cp("act", raw[P, a:b], ps3[bi][P, 0:b - a], [ps3[bi]], [raw], waw=(bi == 0))
            ts("dve", raw[P, 0:1], raw[P, 0:1], mfac[P, 0:1], None, ALU.mult, None, [raw, mfac], [raw])
            yield
            if with_out:
                cp("act", shp_sb[P, tile_idx:tile_idx + 1], raw[P, n:n + 1], [raw], [shp_sb], waw=False)
                cp("act", shs_sb[P, tile_idx, :], raw[P, n + 1:n + 1 + ns], [raw], [shs_sb], waw=False)
            tt("dve", tmpA[P, 0:n], raw[P, 0:n], raw[P, 1:n + 1], ALU.subtract, [raw], [tmpA])
            yield
            stt(out_ap_fn(0, n), tmpA[P, 0:n], mu_ap, raw[P, 1:n + 1], ALU.mult, ALU.add, [tmpA, raw, pc], [obuf], waw=False)
            yield
            if ns:
                fw.dma("sp", sst[0:ns, 0:nparts], sshift[:, col_c0 - C_R:col_c0 - C_R + nparts], reads=[sshift], writes=[sst])
                fw.op("pe", lambda: T.transpose(WK[0][P, 0:ns], sst[0:ns, 0:nparts], ident[0:ns, 0:ns]),
                      reads=[sst, ident], writes=[WK[0]])
                tt("dve", tmpA[P, n:n + ns], WK[0][P, 0:ns], raw[P, n + 1:n + 1 + ns], ALU.subtract, [WK[0], raw], [tmpA])
                stt(out_ap_fn(n, n + ns), tmpA[P, n:n + ns], mu_ap, raw[P, n + 1:n + 1 + ns], ALU.mult, ALU.add, [tmpA, raw, pc], [obuf], waw=False)
            yield

        def rwkv_phase(hT, n, ns, with_out, mfac, hfac, is_last):
            nf = n + ns
            tot = 1 + nf
            for li, (c0, mucol, dst, func) in enumerate(((C_WD, PC_MUWD, thw, AF.Tanh), (C_AD, PC_MUAD, adx, AF.Identity))):
                wt = load_w(w_in, c0, 96)
                for bi, (a, b) in enumerate(blocks(0, tot)):
                    for k in range(16):
                        mm(PJ[bi][0:96, 0:b - a], wt[:, k, 0:96], hT[:, k, a:b], k == 0, k == 15, [wt, hT], [PJ[bi]])
                for _ in shifted(PJ, n, ns, pc[0:96, mucol:mucol + 1], tmpB, lambda a, b: tmpB[0:96, a:b], 48 + li, c0, is_last, mfac, nparts=96):
                    pass
                act(dst[:, 0:nf], tmpB[0:96, 0:nf], func, [tmpB], [dst])
            def F(hp):
                AR, BKt, vT, eL, lw, bonus, zbS, ys16 = SETS[hp % 2]
                w2b = load_small(w2, hp); a2b = load_small(a2, hp)
                wr = load_w(w_in, C_R + 128 * hp) if with_out else None
                wk = load_w(w_in, C_K + 128 * hp)
                wv = load_w(w_in, C_V + 128 * hp)
                wzb = load_w(w_in, C_ZB + 128 * hp) if with_out else None
                col = slice(hp, hp + 1)
                yield
                if with_out:
                    inproj(PJ, wr, 0, hT, 0, tot)
                    yield from shifted(PJ, n, ns, pc[:, PC_MUR + hp:PC_MUR + hp + 1], AR, lambda a, b: AR[:, 1, a:b], hp, C_R + 128 * hp, is_last, mfac)
                yield
                inproj(PJ, wk, 0, hT, 0, tot)
                yield from shifted(PJ, n, ns, pc[:, PC_MUK + hp:PC_MUK + hp + 1], BKt, lambda a, b: BKt[:, 1, a:b], 16 + hp, C_K + 128 * hp, is_last, mfac)
                yield
                inproj(PJ, wv, 0, hT, 0, tot)
                yield from shifted(PJ, n, ns, pc[:, PC_MUV + hp:PC_MUV + hp + 1], vT, lambda a, b: vT[:, a:b], 32 + hp, C_V + 128 * hp, is_last, mfac)
                yield
                for bi, (a, b) in enumerate(blocks(0, nf)):
                    mm(PJ[bi][:, 0:b - a], w2b[:, :], thw[:, a:b], True, True, [w2b, thw], [PJ[bi]])
                    act(lw[:, a:b], PJ[bi][:, 0:b - a], AF.Sigmoid, [PJ[bi], pc], [lw], bias=pc[:, PC_W0 + hp:PC_W0 + hp + 1], waw=(bi == 0))
                yield
                for bi, (a, b) in enumerate(blocks(0, nf)):
                    mm(PJ[bi][:, 0:b - a], a2b[:, :], adx[:, a:b], True, True, [a2b, adx], [PJ[bi]])
                    act(tmpB[:, a:b], PJ[bi][:, 0:b - a], AF.Sigmoid, [PJ[bi], pc], [tmpB], bias=pc[:, PC_A0 + hp:PC_A0 + hp + 1], waw=(bi == 0))
                if with_out:
                    inproj(PJ, wzb, 0, hT, 1, tot)
                    for bi, (a, b) in enumerate(blocks(0, nf)):
                        act(zbS[:, a:b], PJ[bi][:, 0:b - a], AF.Silu, [PJ[bi]], [zbS], waw=(bi == 0))
                yield
                ts("dve", AR[:, 0, 0:nf], BKt[:, 1, 0:nf], pc[:, PC_KK + hp:PC_KK + hp + 1], None, ALU.mult, None, [BKt, pc], [AR], waw=False)
                yield
                tt("dve", tmpA[:, 0:nf], AR[:, 0, 0:nf], AR[:, 0, 0:nf], ALU.mult, [AR], [tmpA])
                yield
                for bi, (a, b) in enumerate(blocks(0, nf)):
                    mm(PJ[bi][:, 0:b - a], bones[:, :], tmpA[:, a:b], True, True, [bones, tmpA], [PJ[bi]])
                    ts("dve", tmpC[:, a:b], PJ[bi][:, 0:b - a], 1e-19, None, ALU.max, None, [PJ[bi]], [tmpC], waw=(bi == 0))
                act(tmpC[:, 0:nf], tmpC[:, 0:nf], AF.Ln, [tmpC], [tmpC])
                yield
                act(tmpC[:, 0:nf], tmpC[:, 0:nf], AF.Exp, [tmpC], [tmpC], scale=-0.5)
                yield
                tt("dve", AR[:, 0, 0:nf], AR[:, 0, 0:nf], tmpC[:, 0:nf], ALU.mult, [AR, tmpC], [AR], waw=False)
                yield
                yield
                ts("dve", tmpA[:, 0:nf], tmpB[:, 0:nf], 1.0, pc[:, PC_KA + hp:PC_KA + hp + 1], ALU.subtract, ALU.mult, [tmpB, pc], [tmpA])
                yield
                stt(BKt[:, 1, 0:nf], tmpA[:, 0:nf], 1.0, BKt[:, 1, 0:nf], ALU.add, ALU.mult, [tmpA, BKt], [BKt], waw=False)
                yield
                yield
                tt("dve", BKt[:, 0, 0:nf], AR[:, 0, 0:nf], tmpB[:, 0:nf], ALU.mult, [AR, tmpB], [BKt], waw=False)
                yield
                if with_out:
                    stt(tmpA[:, 0:nf], AR[:, 1, 0:nf], pc[:, PC_RK + hp:PC_RK + hp + 1], BKt[:, 1, 0:nf], ALU.mult, ALU.mult, [AR, BKt, pc], [tmpA])
                    for bi, (a, b) in enumerate(blocks(0, nf)):
                        mm(PJ[bi][:, 0:b - a], bones[:, :], tmpA[:, a:b], True, True, [bones, tmpA], [PJ[bi]])
                        tt("dve", bonus[:, a:b], PJ[bi][:, 0:b - a], vT[:, a:b], ALU.mult, [PJ[bi], vT], [bonus], waw=(bi == 0))
                yield
                fw.op("dve", lambda: V.tensor_tensor_scan(Lc[:, 0:n], rmask[:, 0:n], lw[:, 0:n], 0.0, ALU.mult, ALU.add), reads=[rmask, lw], writes=[Lc])
                yield
                act(eL[:, 0:n], Lc[:, 0:n], AF.Exp, [Lc], [eL], scale=-EXPM05)
                yield
                tt("dve", tmpA[:, 0:n], Lc[:, 0:n], lw[:, 0:n], ALU.subtract, [Lc, lw], [tmpA])
                yield
                act(tmpA[:, 0:n], tmpA[:, 0:n], AF.Exp, [tmpA], [tmpA], scale=-EXPM05)
                yield
                act(tmpC[:, 0:n], Lc[:, 0:n], AF.Exp, [Lc], [tmpC], scale=EXPM05)
                yield
                if with_out:
                    tt("dve", AR[:, 1, 0:n], AR[:, 1, 0:n], eL[:, 0:n], ALU.mult, [AR, eL], [AR], waw=False)
                stt(AR[:, 0, 0:n], AR[:, 0, 0:n], -1.0, tmpA[:, 0:n], ALU.mult, ALU.mult, [AR, tmpA], [AR], waw=False)
                yield
                tt("dve", BKt[:, 0, 0:n], BKt[:, 0, 0:n], tmpC[:, 0:n], ALU.mult, [BKt, tmpC], [BKt], waw=False)
                yield
                tt("dve", BKt[:, 1, 0:n], BKt[:, 1, 0:n], tmpC[:, 0:n], ALU.mult, [BKt, tmpC], [BKt], waw=False)
                yield
                if with_out and ns:
                    so = slice(n, n + ns)
                    fw.dma("sp", Hs[:], swkv[hp], reads=[swkv], writes=[Hs])
                    act(ws16[:, 0:ns], lw[:, so], AF.Exp, [lw], [ws16], scale=-EXPM05)
                    bc = lambda ap: ap.unsqueeze(2).to_broadcast([128, ns, 64])
                    flat = lambda b_: b_[:].rearrange("p s i -> p (s i)")
                    def bsum_ev(src, ones_, other_fn, other_bufs):
                        for hb in range(2):
                            mm(WK[4][:, :], ones_[:, :], flat(src)[:, hb * 512:(hb + 1) * 512], True, True, [ones_, src], [WK[4]])
                            tt("dve", hv(st2, hb), WK[4][:, :].rearrange("p (s i) -> p s i", i=64), other_fn(hb), ALU.mult, [WK[4]] + other_bufs, [st2], waw=(hb == 0))
                    hv = lambda b_, hb: b_[:, hb * 8:(hb + 1) * 8, :]
                    bc8 = lambda ap, hb: ap[:, hb * 8:(hb + 1) * 8].unsqueeze(2).to_broadcast([128, 8, 64])
                    kk_s, r_s, b_s_, kf_s, v_s, w_s = AR[:, 0, so], AR[:, 1, so], BKt[:, 0, so], BKt[:, 1, so], vT[:, so], ws16[:, 0:ns]
                    tt("dve", st1b[:], Hs[:], bc(kk_s), ALU.mult, [Hs, AR], [st1b])
                    yield
                    bsum_ev(st1b, bones_bf, lambda hb: bc8(b_s_, hb), [BKt])
                    yield
                    tt("dve", Hs[:], Hs[:], bc(w_s), ALU.mult, [Hs, ws16], [Hs])
                    tt("dve", Hs[:], Hs[:], st2[:], ALU.subtract, [Hs, st2], [Hs])
                    tt("dve", st1[:], ident2[:].unsqueeze(1).to_broadcast([128, ns, 64]), bc(v_s), ALU.mult, [ident2, vT], [st1])
                    yield
                    bsum_ev(st1, bones, lambda hb: bc8(kf_s, hb), [BKt])
                    yield
                    tt("dve", Hs[:], Hs[:], st2[:], ALU.add, [Hs, st2], [Hs])
                    fw.dma("sp", wkv_s[hp], Hs[:], reads=[Hs], writes=[wkv_s], waw=False)
                    tt("dve", st1b[:], Hs[:], bc(r_s), ALU.mult, [Hs, AR], [st1b])
                    yield
                    bsum_ev(st1b, bones_bf, lambda hb: ident2[:].unsqueeze(1).to_broadcast([128, 8, 64]), [ident2])
                    yield
                    fw.op("dve", lambda: V.tensor_reduce(ys16[:, 0:ns], st2[:], AX.X, ALU.add), reads=[st2], writes=[ys16])
                    yield
            def CT(hp):
                AR, BKt, vT, eL, lw, bonus, zbS, ys16 = SETS[hp % 2]
                tmpA, tmpB, tmpC = tA2, tB2, tC2
                NC8 = n // NCH
                tkC = [Buf(tk8.t) for _ in range(8)]; scC = [Buf(sc8.t) for _ in range(8)]; qpC = [Buf(qp8.t) for _ in range(8)]
                awC = [Buf(AW8.t) for _ in range(8)]; u0C = [Buf(U08.t) for _ in range(8)]; gtC = [Buf(GT8.t) for _ in range(8)]
                zmC = [Buf(ZM8.t) for _ in range(8)]; qyC = [Buf(Qy8.t) for _ in range(8)]; h0C = [Buf(H08.t) for _ in range(9)]
                for lst, par in ((tkC, tk8), (scC, sc8), (qpC, qp8), (awC, AW8), (u0C, U08), (gtC, GT8), (zmC, ZM8), (qyC, Qy8), (h0C, H08)):
                    for bb in lst:
                        bb.w = dict(par.w); bb.r = dict(par.r)
                bankT = [WK[0], WK[1], WK[2], WK[3]]
                ts("dve", H08[:, 0, :], Hcarry[:, hp, :], hfac[:, 0:1], None, ALU.mult, None, [Hcarry, hfac], [h0C[0]])
                for bq in range(NC8 // 2):
                    ps = bankT[bq]
                    first = True
                    for ci in range(2):
                        c = 2 * bq + ci
                        cs = slice(c * NCH, (c + 1) * NCH)
                        for hh in range(2):
                            Pq = slice(64 * hh, 64 * hh + 64)
                            for xi, src in enumerate((vT[Pq, cs], BKt[Pq, 0, cs], BKt[Pq, 1, cs], AR[Pq, 0, cs])):
                                o0 = ci * 256 + xi * 64
                                last = (ci == 1 and hh == 1 and xi == 3)
                                fw.op("pe", lambda src=src, o0=o0, Pq=Pq, ps=ps: T.matmul(ps[Pq, o0:o0 + 64], src, ident[Pq, Pq], start=True, stop=True),
                                      reads=[vT, BKt, AR, ident], writes=[ps], waw=first, inc=last)
                                first = False
                    cp("act", tk8[:, 2 * bq:2 * bq + 2, 0:4, :], ps[:, :].rearrange("p (c x f) -> p c x f", c=2, x=4), [ps], [tkC[2 * bq], tkC[2 * bq + 1]], waw=False)
                    yield
                for c in range(NC8):
                    cs = slice(c * NCH, (c + 1) * NCH)
                    ps = bankT[c % 4]
                    first = True
                    for hh in range(2):
                        Pq = slice(64 * hh, 64 * hh + 64)
                        if with_out:
                            rhsA = AR[Pq, :, cs]; wA = 128
                        else:
                            rhsA = AR[Pq, 0, cs]; wA = 64
                        fw.op("pe", lambda Pq=Pq, rhsA=rhsA, wA=wA, ps=ps, cs=cs: T.matmul(ps[Pq, 0:wA], BKt[Pq, 0, cs], rhsA, start=True, stop=True),
                              reads=[AR, BKt], writes=[ps], waw=first, inc=False)
                        first = False
                        fw.op("pe", lambda Pq=Pq, rhsA=rhsA, wA=wA, ps=ps, cs=cs: T.matmul(ps[Pq, 128:128 + wA], BKt[Pq, 1, cs], rhsA, start=True, stop=True),
                              reads=[AR, BKt], writes=[ps], waw=False, inc=False)
                        fw.op("pe", lambda Pq=Pq, ps=ps, cs=cs: T.matmul(ps[Pq, 256:320], AR[Pq, 0, cs], BKt[Pq, 0, cs], start=True, stop=True),
                              reads=[AR, BKt], writes=[ps], waw=False, inc=(hh == 1))
                    tt("dve", sc8[:, c, :, :], ps[:, 0:320].rearrange("p (x f) -> p x f", f=64), mask5[:], ALU.mult, [ps, mask5], [scC[c]], waw=False)
                    yield
                cp("act", qp8[:, :, 1, :], sc8[:, :, 0, :], scC, qpC, waw=False)
                cp("act", qp8[:, :, 2, :], sc8[:, :, 4, :], scC, qpC, waw=False)
                tt("dve", qp8[:, :, 0, :], sc8[:, :, 0, :], ident2[:].unsqueeze(1).to_broadcast([128, NC8, 64]), ALU.add, scC + [ident2], qpC, waw=False)
                for rd in range(6):
                    for bq in range(NC8 // 2):
                        ps = bankT[bq]
                        first = True
                        for ci in range(2):
                            c = 2 * bq + ci
                            o0 = ci * 192
                            for hh in range(2):
                                Pq = slice(64 * hh, 64 * hh + 64)
                                last = (ci == 1 and hh == 1)
                                if rd == 0:
                                    fw.op("pe", lambda Pq=Pq, c=c, o0=o0, ps=ps: T.matmul(ps[Pq, o0 + 64:o0 + 128], qp8[Pq, c, 2, :], qp8[Pq, c, 1, :], start=True, stop=True),
                                          reads=[qpC[c]], writes=[ps], waw=first, inc=False)
                                elif rd < 5:
                                    fw.op("pe", lambda Pq=Pq, c=c, o0=o0, ps=ps: T.matmul(ps[Pq, o0:o0 + 128], qp8[Pq, c, 2, :], qp8[Pq, c, 0:2, :], start=True, stop=True),
                                          reads=[qpC[c]], writes=[ps], waw=first, inc=False)
                                else:
                                    fw.op("pe", lambda Pq=Pq, c=c, o0=o0, ps=ps: T.matmul(ps[Pq, o0:o0 + 64], qp8[Pq, c, 2, :], qp8[Pq, c, 0, :], start=True, stop=True),
                                          reads=[qpC[c]], writes=[ps], waw=first, inc=last)
                                first = False
                                if rd < 5:
                                    fw.op("pe", lambda Pq=Pq, c=c, o0=o0, ps=ps: T.matmul(ps[Pq, o0 + 128:o0 + 192], qp8[Pq, c, 1, :], qp8[Pq, c, 2, :], start=True, stop=True),
                                          reads=[qpC[c]], writes=[ps], waw=False, inc=last)
                        yield
                        pv = ps[:, 0:384].rearrange("p (c x f) -> p c x f", c=2, x=3)
                        if rd >= 1:
                            tt("dve", qp8[:, 2 * bq:2 * bq + 2, 0, :], qp8[:, 2 * bq:2 * bq + 2, 0, :], pv[:, :, 0, :], ALU.add, [qpC[2 * bq], qpC[2 * bq + 1], ps], [qpC[2 * bq], qpC[2 * bq + 1]], waw=False)
                        if rd < 5:
                            cp("act", qp8[:, 2 * bq:2 * bq + 2, 1:3, :], pv[:, :, 1:3, :], [ps], [qpC[2 * bq], qpC[2 * bq + 1]], waw=False)
                ps = bankT[0]
                first = True
                for c in range(NC8):
                    for hh in range(2):
                        Pq = slice(64 * hh, 64 * hh + 64)
                        last = (c == NC8 - 1 and hh == 1)
                        fw.op("pe", lambda Pq=Pq, c=c, ps=ps: T.matmul(ps[Pq, c * 64:(c + 1) * 64], sc8[Pq, c, 2, :], tk8[Pq, c, 0, :], start=True, stop=True),
                              reads=[scC[c], tkC[c]], writes=[ps], waw=first, inc=last)
                        first = False
                cp("act", tk8[:, :, 4, :], ps[:, :].rearrange("p (c f) -> p c f", f=64), [ps], tkC, waw=False)
                yield
                for bq in range(NC8 // 4):
                    ps = bankT[2 + bq]
                    first = True
                    for ci in range(4):
                        c = 4 * bq + ci
                        o0 = ci * 128
                        for hh in range(2):
                            Pq = slice(64 * hh, 64 * hh + 64)
                            last = (ci == 3 and hh == 1)
                            fw.op("pe", lambda Pq=Pq, c=c, o0=o0, ps=ps: T.matmul(ps[Pq, o0:o0 + 128], qp8[Pq, c, 0, :], tk8[Pq, c, 3:5, :], start=True, stop=True),
                                  reads=[qpC[c], tkC[c]], writes=[ps], waw=first, inc=last)
                            first = False
                    cp("act", AW8[:, 4 * bq:4 * bq + 4, :, :], ps[:, :].rearrange("p (c x f) -> p c x f", c=4, x=2), [ps], awC[4 * bq:4 * bq + 4], waw=False)
                    yield
                for bq in range(NC8 // 2):
                    ps = bankT[bq]
                    first = True
                    for ci in range(2):
                        c = 2 * bq + ci
                        cs = slice(c * NCH, (c + 1) * NCH)
                        o0 = ci * 256
                        for hh in range(2):
                            Pq = slice(64 * hh, 64 * hh + 64)
                            last = (ci == 1 and hh == 1)
                            fw.op("pe", lambda Pq=Pq, c=c, o0=o0, ps=ps: T.matmul(ps[Pq, o0:o0 + 64], tk8[Pq, c, 1, :], AW8[Pq, c, 1, :], start=True, stop=False),
                                  reads=[tkC[c], awC[c]], writes=[ps], waw=first, inc=False)
                            first = False
                            fw.op("pe", lambda Pq=Pq, c=c, o0=o0, ps=ps: T.matmul(ps[Pq, o0:o0 + 64], tk8[Pq, c, 2, :], tk8[Pq, c, 0, :], start=False, stop=True),
                                  reads=[tkC[c]], writes=[ps], waw=False, inc=False)
                            fw.op("pe", lambda Pq=Pq, c=c, o0=o0, ps=ps: T.matmul(ps[Pq, o0 + 192:o0 + 256], AW8[Pq, c, 0, :], tk8[Pq, c, 1, :], start=True, stop=True),
                                  reads=[awC[c], tkC[c]], writes=[ps], waw=False, inc=(last and not with_out))
                            if with_out:
                                fw.op("pe", lambda Pq=Pq, c=c, o0=o0, ps=ps: T.matmul(ps[Pq, o0 + 64:o0 + 128], AW8[Pq, c, 1, :], sc8[Pq, c, 1, :], start=True, stop=False),
                                      reads=[awC[c], scC[c]], writes=[ps], waw=False, inc=False)
                                fw.op("pe", lambda Pq=Pq, c=c, o0=o0, ps=ps: T.matmul(ps[Pq, o0 + 64:o0 + 128], tk8[Pq, c, 0, :], sc8[Pq, c, 3, :], start=False, stop=True),
                                      reads=[tkC[c], scC[c]], writes=[ps], waw=False, inc=False)
                                fw.op("pe", lambda Pq=Pq, c=c, o0=o0, ps=ps: T.matmul(ps[Pq, o0 + 128:o0 + 192], AW8[Pq, c, 0, :], sc8[Pq, c, 1, :], start=True, stop=True),
                                      reads=[awC[c], scC[c]], writes=[ps], waw=False, inc=last)
                    pv = ps[:, 0:512].rearrange("p (c x f) -> p c x f", c=2, x=4)
                    yield
                    c2 = slice(2 * bq, 2 * bq + 2)
                    ecol2 = eL[:, 2 * bq * NCH:(2 * bq + 2) * NCH].rearrange("p (c t) -> p c t", t=NCH)[:, :, NCH - 1:NCH].to_broadcast([128, 2, 64])
                    tt("dve", ZM8[:, c2, 0, :], pv[:, :, 0, :], ecol2, ALU.mult, [ps, eL], [zmC[2 * bq], zmC[2 * bq + 1]], waw=False)
                    tt("dve", GT8[:, c2, :], pv[:, :, 3, :], ident2[:].unsqueeze(1).to_broadcast([128, 2, 64]), ALU.add, [ps, ident2], [gtC[2 * bq], gtC[2 * bq + 1]], waw=False)
                    if with_out:
                        cp("act", ZM8[:, c2, 1, :], pv[:, :, 1, :], [ps], [zmC[2 * bq], zmC[2 * bq + 1]], waw=False)
                        tt("dve", Qy8[:, c2, :], pv[:, :, 2, :], AR[:, 1, 2 * bq * NCH:(2 * bq + 2) * NCH].rearrange("p (c t) -> p c t", t=NCH), ALU.add, [ps, AR], [qyC[2 * bq], qyC[2 * bq + 1]], waw=False)
                for c in range(NC8):
                    for hh in range(2):
                        Pq = slice(64 * hh, 64 * hh + 64)
                        fw.op("pe", lambda Pq=Pq, c=c: T.matmul(WK[4][Pq, c * 64:(c + 1) * 64], GT8[Pq, c, :], H08[Pq, c, :], start=True, stop=True),
                              reads=[gtC[c], h0C[c]], writes=[WK[4]], waw=(hh == 0 and c == 0), inc=(hh == 1))
                    ecol = eL[:, c * NCH + NCH - 1:c * NCH + NCH]
                    stt(H08[:, c + 1, :], WK[4][:, c * 64:(c + 1) * 64], ecol, ZM8[:, c, 0, :], ALU.mult, ALU.add, [WK[4], eL, zmC[c]], [h0C[c + 1]], waw=False)
                    yield
                if with_out:
                    for c in range(NC8):
                        for hh in range(2):
                            Pq = slice(64 * hh, 64 * hh + 64)
                            fw.op("pe", lambda Pq=Pq, c=c: T.matmul(WK[3][Pq, c * 64:(c + 1) * 64], H08[Pq, c, :], Qy8[Pq, c, :], start=True, stop=True),
                                  reads=[h0C[c], qyC[c]], writes=[WK[3]], waw=(hh == 0 and c == 0), inc=(hh == 1 and c == NC8 - 1))
                    tt("dve", YT[:, 0:n].rearrange("p (c t) -> p c t", t=NCH), WK[3][:, :].rearrange("p (c t) -> p c t", t=NCH), ZM8[:, :, 1, :], ALU.add, [WK[3]] + zmC, [YT], waw=False)
                for lst, par in ((tkC, tk8), (scC, sc8), (qpC, qp8), (awC, AW8), (u0C, U08), (gtC, GT8), (zmC, ZM8), (qyC, Qy8), (h0C, H08)):
                    for bb in lst:
                        for kx, vx in bb.w.items():
                            if par.w.get(kx, 0) < vx:
                                par.w[kx] = vx
                        for kx, vx in bb.r.items():
                            if par.r.get(kx, 0) < vx:
                                par.r[kx] = vx
                cp("act", Hst[:], H08[:, NC8, :], [H08], [Hst])
                cp("act", Hcarry[:, hp, :], Hst[:], [Hst], [Hcarry], waw=False)
                if not with_out:
                    return
                if is_last:
                    fw.dma("sp", wkv_p[hp], Hcarry[:, hp, :], reads=[Hcarry], writes=[wkv_p], waw=False)
                yield
                if ns:
                    cp("dve", YT[:, n:n + ns], ys16[:, 0:ns], [ys16], [YT], waw=False)
                for bi, (a, b) in enumerate(blocks(0, nf)):
                    mm(PJ[bi][:, 0:b - a], bones64[:, :], YT[:, a:b], True, True, [bones64, YT], [PJ[bi]])
                    tt("dve", tmpA[:, a:b], YT[:, a:b], PJ[bi][:, 0:b - a], ALU.subtract, [YT, PJ[bi]], [tmpA], waw=(bi == 0))
                tt("dve", tmpB[:, 0:nf], tmpA[:, 0:nf], tmpA[:, 0:nf], ALU.mult, [tmpA], [tmpB])
                for bi, (a, b) in enumerate(blocks(0, nf)):
                    mm(PJ[bi][:, 0:b - a], bones64[:, :], tmpB[:, a:b], True, True, [bones64, tmpB], [PJ[bi]])
                    act(tmpC[:, a:b], PJ[bi][:, 0:b - a], AF.Ln, [PJ[bi]], [tmpC], bias=gneps[:, 0:1], waw=(bi == 0))
                act(tmpC[:, 0:nf], tmpC[:, 0:nf], AF.Exp, [tmpC], [tmpC], scale=-0.5)
                tt("dve", tmpA[:, 0:nf], tmpA[:, 0:nf], tmpC[:, 0:nf], ALU.mult, [tmpA, tmpC], [tmpA])
                ts("dve", tmpA[:, 0:nf], tmpA[:, 0:nf], pc[:, PC_GNG + hp:PC_GNG + hp + 1], pc[:, PC_GNB + hp:PC_GNB + hp + 1], ALU.mult, ALU.add, [tmpA, pc], [tmpA])
                tt("dve", tmpA[:, 0:nf], tmpA[:, 0:nf], bonus[:, 0:nf], ALU.add, [tmpA, bonus], [tmpA])
                tt("dve", yb_T[:, hp, 0:nf], tmpA[:, 0:nf], zbS[:, 0:nf], ALU.mult, [tmpA, zbS], [yb_T], waw=False)

            def interleave(g1, g2):
                gens = [g1, g2]
                while gens:
                    for g in list(gens):
                        try:
                            next(g)
                        except StopIteration:
                            gens.remove(g)
            for _ in F(0):
                pass
            for hp in range(16):
                g1 = CT(hp)
                g2 = F(hp + 1) if hp < 15 else iter(())
                interleave(g1, g2)
        def fin():
            fw.dma("sp", sh_p[:], shp_sb[:], reads=[shp_sb], writes=[sh_p])
            fw.dma("sp", sh_s[:], shs_sb[:], reads=[shs_sb], writes=[sh_s])
        return rwkv_phase, fin
    hT = fw.sb("hT", [128, 16, 1 + NF], BF16)
    yb_T = fw.sb("yb_T", [128, 16, NF], BF16)
    Hcarry = fw.sb("Hcarry", [128, 16, 64])
    fw.op("pool", lambda: G.memset(Hcarry[:], 0.0), writes=[Hcarry])
    fw.op("dve", lambda: V.memset(hT[:, :, 0:1], 0.0), writes=[hT])

    def gmo_segment(seg, ns):
        nf = TO + ns
        nch = 4 + (1 if ns else 0)
        bank5 = [PJ[0], PJ[1], PJ[2], WK[0], WK[1]]
        tA = fw.sb("tA", [128, NF]); tB = fw.sb("tB", [128, NF]); tC = fw.sb("tC", [128, NF])
        vn = fw.sb("vn", [128, 5, 1024], BF16)
        ya_T = fw.sb("ya_T", [128, 8, NF], BF16)
        mg_T = fw.sb("mg_T", [128, 16, NF], BF16)
        mG = fw.mark()
        lng = fw.sb("lng", [128, 1024]); lnb = fw.sb("lnb", [128, 1024]); bsb = fw.sb("bsb", [128, 1024])
        fw.dma("sp", lng[:], lnv_g[0:1, :].to_broadcast([128, 1024]), reads=[lnv_g], writes=[lng])
        fw.dma("sp", lnb[:], lnv_b[0:1, :].to_broadcast([128, 1024]), reads=[lnv_b], writes=[lnb])
        fw.dma("sp", bsb[:], b_s[0:1, :].to_broadcast([128, 1024]), reads=[b_s], writes=[bsb])
        wsc = fw.sb("wsc", [16, 16])
        fw.dma("sp", wsc[:, 0:8], ws00[0:1, :].to_broadcast([16, 8]), reads=[ws00], writes=[wsc])
        fw.dma("sp", wsc[:, 8:16], bs0[0:1, :].to_broadcast([16, 8]), reads=[bs0], writes=[wsc], waw=False)
        wm = fw.sb("wm", [128, 8, 128], BF16)
        gtmp = fw.sb("gtmp", [128, 1024]); gtmp2 = fw.sb("gtmp2", [128, 1024]); stats = fw.sb("stats", [128, 16])
        vg5 = [fw.sb(f"vg5_{i}", [128, 1024]) for i in range(nch)]
        fw.dma("sp", gtmp[:, :].rearrange("p (g t) -> p g t", t=128), w_sT[:].rearrange("g s t -> s g t"), reads=[w_sT], writes=[gtmp])
        tt("dve", wm[:], gtmp[:, :].rearrange("p (g t) -> p g t", t=128), triu[:].unsqueeze(1).to_broadcast([128, 8, 128]), ALU.mult, [gtmp, triu], [wm])
        mixs = fw.sb("mixs", [16, 1024]); mixsT = fw.sb("mixsT", [128, 8, 16])

        def gelu_from(src_ap, n, width, R, outbuf, out_ap):
            P = slice(0, n)
            act(gtmp[P, 0:width], src_ap, AF.Square, R, [gtmp])
            ts("dve", gtmp[P, 0:width], gtmp[P, 0:width], 0.044715, 1.0, ALU.mult, ALU.add, [gtmp], [gtmp])
            tt("dve", gtmp[P, 0:width], gtmp[P, 0:width], src_ap, ALU.mult, R + [gtmp], [gtmp])
            act(gtmp[P, 0:width], gtmp[P, 0:width], AF.Sigmoid, [gtmp], [gtmp], scale=1.5957691216057308)
            tt("dve", out_ap, gtmp[P, 0:width], src_ap, ALU.mult, R + [gtmp], [outbuf], waw=False)

        for cb in range(2):
            for q in range(4):
                wt = load_w(w_in, 1024 + cb * 512 + q * 128)
                for ci in range(nch):
                    n = 128 if ci < 4 else 16
                    c0 = 1 + 128 * ci
                    for k in range(16):
                        mm(bank5[ci][0:n, q * 128:(q + 1) * 128], hT[:, k, c0:c0 + n], wt[:, k, :], k == 0, k == 15, [hT, wt], [bank5[ci]])
            for ci in range(nch):
                n = 128 if ci < 4 else 16
                gelu_from(bank5[ci][0:n, :], n, 512, [bank5[ci]], vg5[ci], vg5[ci][0:n, cb * 512:(cb + 1) * 512])
        for ci in range(nch):
            n = 128 if ci < 4 else 16
            P = slice(0, n)
            g2 = vg5[ci]
            for cb in range(2):
                fw.op("dve", lambda cb=cb: V.bn_stats(stats[P, cb * 6:(cb + 1) * 6], g2[P, cb * 512:(cb + 1) * 512]), reads=[g2], writes=[stats], waw=False)
            fw.op("dve", lambda: V.bn_aggr(stats[P, 12:14], stats[P, 0:12]), reads=[stats], writes=[stats], waw=False)
            act(stats[P, 14:15], stats[P, 13:14], AF.Sqrt, [stats, neps], [stats], bias=neps[P, 0:1], waw=False)
            fw.op("dve", lambda: V.reciprocal(stats[P, 15:16], stats[P, 14:15]), reads=[stats], writes=[stats], waw=False)
            ts("dve", g2[P, :], g2[P, :], stats[P, 12:13], stats[P, 15:16], ALU.subtract, ALU.mult, [g2, stats], [g2])
            tt("dve", g2[P, :], g2[P, :], lng[P, :], ALU.mult, [g2, lng], [g2])
            if ci < 4:
                tt("dve", vn[:, ci, :], g2[:, :], lnb[:, :], ALU.add, [g2, lnb], [vn], waw=False)
            else:
                tt("dve", g2[P, :], g2[P, :], lnb[P, :], ALU.add, [g2, lnb], [g2])
                fw.dma("sp", cv_s[:], g2[0:16, :], reads=[g2], writes=[cv_s])
                g3 = lambda ap: ap.rearrange("p (g c) -> p g c", c=128)
                tt("dve", g3(mixs[:, :]), g3(g2[0:16, :]), wsc[:, 0:8].unsqueeze(2).to_broadcast([16, 8, 128]), ALU.mult, [g2, wsc], [mixs])
                tt("dve", g3(mixs[:, :]), g3(mixs[:, :]), wsc[:, 8:16].unsqueeze(2).to_broadcast([16, 8, 128]), ALU.add, [mixs, wsc], [mixs])
                for g in range(8):
                    fw.op("pe", lambda g=g: T.transpose(WK[2][:, g * 16:(g + 1) * 16], mixs[0:16, g * 128:(g + 1) * 128], ident[0:16, 0:16]),
                          reads=[mixs, ident], writes=[WK[2]], waw=(g == 0), inc=(g == 7))
                cp("dve", mixsT[:], WK[2][:, 0:128].rearrange("p (g s) -> p g s", s=16), [WK[2]], [mixsT])
        for g in range(8):
            wu = load_w(w_in, 128 * g); wz = load_w(w_in, 2048 + 128 * g)
            for ci in range(4):
                mm(WK[0][:, ci * 128:(ci + 1) * 128], vn[:, ci, g * 128:(g + 1) * 128], wm[:, g, :], True, True, [vn, wm], [WK[0]])
            tt("dve", gtmp2[:, 0:512].rearrange("p (c t) -> p c t", t=128), WK[0][:, :].rearrange("p (c t) -> p c t", t=128),
               bsb[:, g * 128:(g + 1) * 128].unsqueeze(1).to_broadcast([128, 4, 128]), ALU.add, [WK[0], bsb], [gtmp2])
            inproj(PJ, wu, 0, hT, 1, 1 + nf)
            for bi, (a, b) in enumerate(blocks(0, nf)):
                gelu_from(PJ[bi][:, 0:b - a], 128, b - a, [PJ[bi]], tB, tB[:, a:b])
            tt("dve", tB[:, 0:TO], tB[:, 0:TO], gtmp2[:, 0:TO], ALU.mult, [tB, gtmp2], [tB])
            if ns:
                tt("dve", tB[:, TO:nf], tB[:, TO:nf], mixsT[:, g, :], ALU.mult, [tB, mixsT], [tB])
            inproj(PJ, wz, 0, hT, 1, 1 + nf)
            for bi, (a, b) in enumerate(blocks(0, nf)):
                act(tC[:, a:b], PJ[bi][:, 0:b - a], AF.Silu, [PJ[bi]], [tC], waw=(bi == 0))
            tt("dve", ya_T[:, g, 0:nf], tB[:, 0:nf], tC[:, 0:nf], ALU.mult, [tB, tC], [ya_T], waw=False)
        fw.release(mG)
        for dt_ in range(16):
            wga = load_w(w_in, C_GA + 128 * dt_)
            inproj(PJ, wga, 0, hT, 1, 1 + nf)
            for bi, (a, b) in enumerate(blocks(0, nf)):
                act(tA[:, a:b], PJ[bi][:, 0:b - a], AF.Sigmoid, [PJ[bi]], [tA], waw=(bi == 0))
            wpa = load_w(p_a, 128 * dt_, 128, rows=1024)
            inproj(PJ, wpa, 0, ya_T, 0, nf, nk=8)
            for bi, (a, b) in enumerate(blocks(0, nf)):
                tt("dve", tA[:, a:b], tA[:, a:b], PJ[bi][:, 0:b - a], ALU.mult, [tA, PJ[bi]], [tA], waw=False)
            wgb = load_w(w_in, C_GB + 128 * dt_)
            inproj(PJ, wgb, 0, hT, 1, 1 + nf)
            for bi, (a, b) in enumerate(blocks(0, nf)):
                act(tB[:, a:b], PJ[bi][:, 0:b - a], AF.Sigmoid, [PJ[bi]], [tB], waw=(bi == 0))
            wpb = load_w(p_b, 128 * dt_)
            inproj(PJ, wpb, 0, yb_T, 0, nf)
            for bi, (a, b) in enumerate(blocks(0, nf)):
                tt("dve", tB[:, a:b], tB[:, a:b], PJ[bi][:, 0:b - a], ALU.mult, [tB, PJ[bi]], [tB], waw=False)
            tt("dve", mg_T[:, dt_, 0:nf], tA[:, 0:nf], tB[:, 0:nf], ALU.add, [tA, tB], [mg_T], waw=False)
        mO = fw.mark()
        fgb = fw.sb("fgb", [128, D])
        fw.dma("sp", fgb[:], final_g[0:1, :].to_broadcast([128, D]), reads=[final_g], writes=[fgb])
        ob5 = [fw.sb(f"ob5_{i}", [128, D]) for i in range(nch)]
        alloc_x()
        xt = XH["xt"]; junk = XH["junk"]; rs_ring = XH["rs"]
        for cb in range(4):
            for q in range(4):
                wt = load_w(w_out, 512 * cb + 128 * q)
                for ci in range(nch):
                    n = 128 if ci < 4 else 16
                    c0 = 128 * ci
                    for k in range(16):
                        mm(bank5[ci][0:n, q * 128:(q + 1) * 128], mg_T[:, k, c0:c0 + n], wt[:, k, :], k == 0, k == 15, [mg_T, wt], [bank5[ci]])
            for ci in range(nch):
                n = 128 if ci < 4 else 16
                P = slice(0, n)
                gsrc, gb_ = (gate_bc[P, cb * 512:(cb + 1) * 512], gate_bc) if ci < 4 else (gate_s[0:16, cb * 512:(cb + 1) * 512], gate_s)
                tt("dve", ob5[ci][P, cb * 512:(cb + 1) * 512], bank5[ci][P, :], gsrc, ALU.mult, [bank5[ci], gb_], [ob5[ci]], waw=False)
        for ci in range(nch):
            n = 128 if ci < 4 else 16
            P = slice(0, n)
            c0 = 128 * ci
            ob = ob5[ci]
            if ci < 4:
                fw.dma("sp", xt[P, :], x_own[seg * TO + c0:seg * TO + c0 + n, :], reads=[x_own], writes=[xt])
            else:
                fw.dma("sp", xt[P, :], x_misc[3, 0:16, :], reads=[x_misc], writes=[xt])
            tt("dve", ob[P, :], ob[P, :], xt[P, :], ALU.add, [ob, xt], [ob])
            rs = rs_ring[ci % 2]
            fw.op("act", lambda: A.activation(junk[P, :], ob[P, :], AF.Square, accum_out=rs[P, 0:1]), reads=[ob], writes=[junk, rs])
            ts("dve", rs[P, 1:2], rs[P, 0:1], 1.0 / D, NORM_EPS, ALU.mult, ALU.add, [rs], [rs])
            act(rs[P, 2:3], rs[P, 1:2], AF.Sqrt, [rs], [rs])
            fw.op("dve", lambda: V.reciprocal(rs[P, 3:4], rs[P, 2:3]), reads=[rs], writes=[rs])
            stt(ob[P, :], ob[P, :], rs[P, 3:4], fgb[P, :], ALU.mult, ALU.mult, [ob, rs, fgb], [ob])
            if ci < 4:
                fw.dma("sp", y_own[seg * TO + c0:seg * TO + c0 + n, :], ob[P, :], reads=[ob], writes=[y_own], waw=False)
            else:
                fw.dma("sp", y_s[:], ob[P, :], reads=[ob], writes=[y_s], waw=False)
        fw.release(mO)

    for ph in ("P", "R"):
        for seg in range(2):
            if STOP == "P" and ph == "R":
                return done()
            if STOP == "R0" and ph == "R" and seg == 1:
                return done()
            ns = TS if (ph == "R" and seg == 1) else 0
            xsrc = x_pre if ph == "P" else x_own
            mh = fw.mark()
            alloc_x(2)
            for i in range(4):
                try:
                    build_hT(hT, 1 + 128 * i, xsrc, seg * TO + 128 * i, 128)
                except StopBuild:
                    return done()
                if STOP == "h1":
                    return done()
            if STOP == "h4":
                return done()
            build_hT(hT, 1 + TO, x_misc, 0, 17, is_misc=True, ns=ns, xsel=(0 if ph == "P" else 2) + seg)
            if STOP == "h":
                return done()
            fw.release(mh)
            mfac = {("P", 0): zero1, ("P", 1): one1, ("R", 0): mk, ("R", 1): one1}[(ph, seg)]
            hfac = mk if (ph == "R" and seg == 0) else one1
            m1 = fw.mark()
            rp, fin = make_rwkv()
            rp(hT, TO, ns, ph == "R", mfac, hfac, ph == "R" and seg == 1)
            if ph == "R" and seg == 1:
                fin()
            fw.release(m1)
            if STOP == "P0":
                return done()
            if ph == "R" and STOP != "nogmo":
                m2 = fw.mark()
                gmo_segment(seg, ns)
                fw.release(m2)
    fw.finish()
    fw.close()
    return nc


_CACHE = {}


def kernel(x_prompt, x_sample, c_prompt, c_sample, state_wkv, state_shift,
           norm_g, w_c, b_c, w_in, ln_v_g, ln_v_b, w_s, b_s,
           mu_shift, w0, w2, a0, a2, k_k, k_a, r_k, gn_g, gn_b,
           p_a, p_b, w_out, final_g):
    f = lambda a: np.ascontiguousarray(np.asarray(a, dtype=np.float32))
    x_prompt, x_sample, c_prompt, c_sample = f(x_prompt), f(x_sample), f(c_prompt), f(c_sample)
    state_wkv, state_shift = f(state_wkv), f(state_shift)
    if "nc" not in _CACHE:
        _CACHE["nc"] = build_program()
    nc = _CACHE["nc"]
    mu = f(mu_shift)[0]
    t16 = lambda v: f(v).reshape(16, 128).T
    pc = np.zeros((128, NPC), np.float32)
    cols = [f(norm_g)[0], mu[0:2048], mu[2144:4192], mu[4192:6240], f(w0)[0], f(a0)[0], f(k_k)[0], f(k_a)[0],
            f(r_k)[0].reshape(-1), f(gn_g)[0], f(gn_b)[0]]
    for i, v in enumerate(cols):
        pc[:, 16 * i:16 * i + 16] = t16(v)
    pc[0:96, PC_MUWD] = mu[2048:2144]
    pc[0:96, PC_MUAD] = mu[6240:6336]
    shared = {
        "pc": pc, "w_c": f(w_c)[0], "b_c": f(b_c)[0][None, :], "w_in": f(w_in)[0],
        "ln_v_g": f(ln_v_g)[0][None, :], "ln_v_b": f(ln_v_b)[0][None, :],
        "w_sT": np.ascontiguousarray(f(w_s)[0].transpose(0, 2, 1)), "b_s": f(b_s)[0].reshape(1, 1024),
        "ws00": np.ascontiguousarray(f(w_s)[0][:, 0, 0])[None, :], "bs0": np.ascontiguousarray(f(b_s)[0][:, 0])[None, :],
        "w2": f(w2)[0], "a2": f(a2)[0], "p_a": f(p_a)[0], "p_b": f(p_b)[0], "w_out": f(w_out)[0],
        "final_g": f(final_g)[None, :],
    }
    in_maps = []
    for c in range(8):
        b, hh = c // 2, c % 2
        xs = x_sample[16 * c:16 * c + 16, 0]
        x_pre = x_prompt[b, 0:1024]
        x_own = x_prompt[b, 1024 * hh:1024 * hh + 1024]
        xm = np.zeros((4, 17, 2048), np.float32)
        xm[1, 16] = x_pre[511]
        xm[2, 16] = x_pre[1023]
        xm[3, 0:16] = xs
        xm[3, 16] = x_own[511]
        cin = np.concatenate([c_sample[16 * c:16 * c + 16], c_prompt[b:b + 1]], 0)
        sw = state_wkv[0, 16 * c:16 * c + 16].reshape(16, 16, 2, 64, 64)
        sw = np.ascontiguousarray(sw.transpose(1, 2, 4, 0, 3)).reshape(16, 128, 16, 64)
        m = dict(shared)
        m.update({"x_pre": np.ascontiguousarray(x_pre), "x_own": np.ascontiguousarray(x_own), "x_misc": xm,
                  "cin": np.ascontiguousarray(cin), "maskv": np.full((128, 1), float(hh), np.float32),
                  "swkv": sw, "sshift": np.ascontiguousarray(state_shift[0, 16 * c:16 * c + 16])})
        in_maps.append(m)
    res = run_bass_kernel_spmd(nc, in_maps, core_ids=list(range(8))).results
    y_prompt = np.zeros((4, 2048, 2048), np.float32)
    y_sample = np.zeros((128, 1, 2048), np.float32)
    wkv_prompt = np.zeros((1, 4, 32, 64, 64), np.float32)
    shift_prompt = np.zeros((1, 4, 6336), np.float32)
    wkv_sample = np.zeros((1, 128, 32, 64, 64), np.float32)
    shift_sample = np.zeros((1, 128, 6336), np.float32)
    cv = np.zeros((1, 128, 1, 1024), np.float32)

    def unshift(a):
        tail = a.shape[2:]
        r = a[:, 0:16].transpose(1, 0, *range(2, a.ndim)).reshape(2048, *tail)
        k = a[:, 16:32].transpose(1, 0, *range(2, a.ndim)).reshape(2048, *tail)
        v = a[:, 32:48].transpose(1, 0, *range(2, a.ndim)).reshape(2048, *tail)
        wd = a[0:96, 48]
        ad = a[0:96, 49]
        return np.concatenate([r, wd, k, v, ad], 0)

    for c in range(8):
        b, hh = c // 2, c % 2
        r = res[c]
        y_prompt[b, 1024 * hh:1024 * hh + 1024] = r["y_own"]
        y_sample[16 * c:16 * c + 16, 0] = r["y_s"]
        ws = r["wkv_s"].reshape(16, 2, 64, 16, 64)
        wkv_sample[0, 16 * c:16 * c + 16] = ws.transpose(3, 0, 1, 4, 2).reshape(16, 32, 64, 64)
        shift_sample[0, 16 * c:16 * c + 16] = unshift(r["sh_s"]).T
        cv[0, 16 * c:16 * c + 16, 0] = r["cv_s"]
        if hh == 1:
            wp = r["wkv_p"].reshape(16, 2, 64, 64)
            wkv_prompt[0, b] = wp.transpose(0, 1, 3, 2).reshape(32, 64, 64)
            shift_prompt[0, b] = unshift(r["sh_p"])
    return (y_prompt, y_sample, wkv_prompt, shift_prompt, wkv_sample, shift_sample, cv)
```
